# Optimizing a Trainium2 kernel written in Bass

```python
import jax, jax.numpy as jnp
from jax import lax
import numpy as np

D_MODEL = 1024
BATCH = 32
SEQ = 256
DEPTH = 1
DEC_BATCH = 4
DEC_SEQ = 4096
PAST_LEN = 512

GRID_W = 64
D_MIX = 1024
HEAD_DIM = 64
N_ATTN_HEADS = 8
N_KV_HEADS = 2
KV_GROUP = N_ATTN_HEADS // N_KV_HEADS
ATTN_WIDTH = N_ATTN_HEADS * HEAD_DIM
KV_WIDTH = N_KV_HEADS * HEAD_DIM
WINDOW = 128
BLOCK = 128
N_GLA_HEADS = 4
GLA_DK = 64
GLA_DV = 128
GLA_K_WIDTH = N_GLA_HEADS * GLA_DK
GLA_V_WIDTH = N_GLA_HEADS * GLA_DV
GATE_RANK = 16
GATE_NORMALIZER = 16.0
CHUNK = 64
ROPE_BASE = 10000.0
EPS = 1e-6
NEG_INF = -1e30
PROJ_SIZES = (ATTN_WIDTH, KV_WIDTH, KV_WIDTH, ATTN_WIDTH, GLA_K_WIDTH, GLA_K_WIDTH, GLA_V_WIDTH, GATE_RANK, GATE_RANK, GLA_V_WIDTH)
D_IN = 2848

kernel_name = "hymba_window_gqa_gla_prefix_dit_step"


def rmsnorm(x, g):
    xf = x.astype(jnp.float32)
    xf = xf * lax.rsqrt(jnp.mean(xf * xf, axis=-1, keepdims=True) + EPS)
    return xf.astype(x.dtype) * g


def adaln(cvec, w_mod, b_mod):
    m = jax.nn.silu(cvec) @ w_mod + b_mod
    shift, scale, gate = jnp.split(m, 3, axis=-1)
    return shift[:, None, :], scale[:, None, :], gate[:, None, :]


def axial_rope(x):
    n_tok = x.shape[1]
    rows = n_tok // GRID_W
    r = jnp.repeat(jnp.arange(rows, dtype=jnp.float32), GRID_W)
    col = jnp.tile(jnp.arange(GRID_W, dtype=jnp.float32), rows)
    half = HEAD_DIM // 2
    nf = half // 2
    freqs = ROPE_BASE ** (-jnp.arange(nf, dtype=jnp.float32) / nf)

    def rot(xs, pos):
        ang = pos[:, None] * freqs[None, :]
        cos = jnp.cos(ang)[:, None, :]
        sin = jnp.sin(ang)[:, None, :]
        x1, x2 = xs[..., :nf], xs[..., nf:]
        return jnp.concatenate([x1 * cos - x2 * sin, x2 * cos + x1 * sin], axis=-1)

    xf = x.astype(jnp.float32)
    out = jnp.concatenate([rot(xf[..., :half], r), rot(xf[..., half:], col)], axis=-1)
    return out.astype(x.dtype)


def sink_softmax_attend(q, parts, sink):
    bsz, nq = q.shape[0], q.shape[1]
    scale = HEAD_DIM ** -0.5
    logits = []
    for k, v, mask in parts:
        s = jnp.einsum('bqkgd,bskd->bkgqs', q, k).astype(jnp.float32) * scale
        if mask is not None:
            s = jnp.where(mask, s, NEG_INF)
        logits.append(s)
    sink_l = jnp.broadcast_to(sink.astype(jnp.float32).reshape(1, N_KV_HEADS, KV_GROUP, 1, 1), (bsz, N_KV_HEADS, KV_GROUP, nq, 1))
    p = jax.nn.softmax(jnp.concatenate(logits + [sink_l], axis=-1), axis=-1)
    outs = []
    off = 0
    for k, v, _ in parts:
        n_k = k.shape[1]
        outs.append(jnp.einsum('bkgqs,bskd->bqkgd', p[..., off:off + n_k].astype(v.dtype), v))
        off += n_k
    return sum(outs).reshape(bsz, nq, ATTN_WIDTH)


def context_attention(q, k, v, sink):
    bsz, n_tok = q.shape[0], q.shape[1]
    nb = n_tok // BLOCK
    qb = q.reshape(bsz, nb, BLOCK, N_KV_HEADS, KV_GROUP, HEAD_DIM).transpose(1, 0, 2, 3, 4, 5)
    out = lax.map(lambda qi: sink_softmax_attend(qi, [(k, v, None)], sink), qb)
    return out.transpose(1, 0, 2, 3).reshape(bsz, n_tok, ATTN_WIDTH)


def latent_attention(q, k, v, k_ctx, v_ctx, sink):
    bsz, n_tok = q.shape[0], q.shape[1]
    nb = n_tok // BLOCK
    qb = q.reshape(bsz, nb, BLOCK, N_KV_HEADS, KV_GROUP, HEAD_DIM).transpose(1, 0, 2, 3, 4, 5)
    pad = ((0, 0), (BLOCK, BLOCK), (0, 0), (0, 0))
    kp = jnp.pad(k, pad).reshape(bsz, nb + 2, BLOCK, N_KV_HEADS, HEAD_DIM)
    vp = jnp.pad(v, pad).reshape(bsz, nb + 2, BLOCK, N_KV_HEADS, HEAD_DIM)

    def window(t):
        return jnp.concatenate([t[:, 0:nb], t[:, 1:nb + 1], t[:, 2:nb + 2]], axis=2).transpose(1, 0, 2, 3, 4)

    blk = jnp.arange(nb)[:, None] * BLOCK
    qpos = blk + jnp.arange(BLOCK)[None, :]
    kpos = blk - BLOCK + jnp.arange(3 * BLOCK)[None, :]
    mask = (jnp.abs(qpos[:, :, None] - kpos[:, None, :]) <= WINDOW) & ((kpos >= 0) & (kpos < n_tok))[:, None, :]

    def block(args):
        qi, ki, vi, mi = args
        return sink_softmax_attend(qi, [(ki, vi, mi), (k_ctx, v_ctx, None)], sink)

    out = lax.map(block, (qb, window(kp), window(vp), mask))
    return out.transpose(1, 0, 2, 3).reshape(bsz, n_tok, ATTN_WIDTH)


def gla_scan(q, k, v, g, s0):
    bsz, n_tok = q.shape[0], q.shape[1]
    n = n_tok // CHUNK

    def rs(t):
        return t.reshape(bsz, n, CHUNK, t.shape[2], t.shape[3]).astype(jnp.float32)

    qf, kf, vf, gf = rs(q), rs(k), rs(v), rs(g)
    gc = jnp.cumsum(gf, axis=2)
    gtot = gc[:, :, -1]
    q_dec = qf * jnp.exp(gc) * (GLA_DK ** -0.5)
    k_inv = kf * jnp.exp(-gc)
    k_end = kf * jnp.exp(gtot[:, :, None] - gc)
    causal = jnp.tril(jnp.ones((CHUNK, CHUNK), dtype=bool))
    a = jnp.where(causal, jnp.einsum('bnihd,bnjhd->bnhij', q_dec, k_inv), 0.0)
    o_intra = jnp.einsum('bnhij,bnjhv->bnihv', a, vf)
    ds = jnp.einsum('bnjhd,bnjhv->bnhdv', k_end, vf)

    def step(s, xs):
        decay, d = xs
        return jnp.exp(decay)[..., None] * s + d, s

    s_fin, s_prev = lax.scan(step, s0.astype(jnp.float32), (gtot.transpose(1, 0, 2, 3), ds.transpose(1, 0, 2, 3, 4)))
    o_inter = jnp.einsum('bnihd,nbhdv->bnihv', q_dec, s_prev)
    o = (o_inter + o_intra).reshape(bsz, n_tok, N_GLA_HEADS, GLA_DV)
    return o.astype(v.dtype), s_fin.astype(v.dtype)


def gla_bidir(q, k, v, g_f, g_b, s0_f, s0_b):
    o_f, s_f = gla_scan(q, k, v, g_f, s0_f)
    o_b, s_b = gla_scan(q[:, ::-1], k[:, ::-1], v[:, ::-1], g_b[:, ::-1], s0_b)
    return o_f + o_b[:, ::-1], s_f, s_b


def project(x, shift, scale, norm_g, w_in, w_gk_f, b_gk_f, w_gk_b, b_gk_b):
    bsz, n_tok, _ = x.shape
    h = rmsnorm(x, norm_g) * (1 + scale) + shift
    z = h @ w_in
    q, k, v, za, qg, kg, vg, lr_f, lr_b, zg = jnp.split(z, list(np.cumsum(PROJ_SIZES)[:-1]), axis=-1)
    q = q.reshape(bsz, n_tok, N_ATTN_HEADS, HEAD_DIM)
    k = k.reshape(bsz, n_tok, N_KV_HEADS, HEAD_DIM)
    v = v.reshape(bsz, n_tok, N_KV_HEADS, HEAD_DIM)
    qg = qg.reshape(bsz, n_tok, N_GLA_HEADS, GLA_DK)
    kg = kg.reshape(bsz, n_tok, N_GLA_HEADS, GLA_DK)
    vg = vg.reshape(bsz, n_tok, N_GLA_HEADS, GLA_DV)
    g_f = (jax.nn.log_sigmoid((lr_f @ w_gk_f + b_gk_f).astype(jnp.float32)) / GATE_NORMALIZER).reshape(bsz, n_tok, N_GLA_HEADS, GLA_DK)
    g_b = (jax.nn.log_sigmoid((lr_b @ w_gk_b + b_gk_b).astype(jnp.float32)) / GATE_NORMALIZER).reshape(bsz, n_tok, N_GLA_HEADS, GLA_DK)
    return q, k, v, za, qg, kg, vg, g_f, g_b, zg


def merge(o_a, za, o_g, zg, gla_norm_g, w_out):
    bsz, n_tok = o_a.shape[0], o_a.shape[1]
    ya = o_a * jax.nn.silu(za)
    yg = (rmsnorm(o_g, gla_norm_g).reshape(bsz, n_tok, GLA_V_WIDTH)) * jax.nn.silu(zg)
    return jnp.concatenate([ya, yg], axis=-1) @ w_out


def setup_inputs(seed: int = 0) -> dict:
    key = jax.random.key(seed)
    ks = jax.random.split(key, 20)
    f32 = jnp.float32
    nrm = lambda k, shape, s: jax.random.normal(k, shape, f32) * s
    return {
        "x_prompt": nrm(ks[0], (BATCH, SEQ, D_MODEL), 1.0),
        "x_sample": nrm(ks[1], (DEC_BATCH, DEC_SEQ, D_MODEL), 1.0),
        "c": nrm(ks[2], (DEC_BATCH, D_MODEL), 1.0),
        "cache_k": nrm(ks[3], (DEC_BATCH, DEPTH, PAST_LEN, N_KV_HEADS, HEAD_DIM), 1.0),
        "cache_v": nrm(ks[4], (DEC_BATCH, DEPTH, PAST_LEN, N_KV_HEADS, HEAD_DIM), 1.0),
        "state_gla": nrm(ks[5], (DEC_BATCH, DEPTH, 2, N_GLA_HEADS, GLA_DK, GLA_DV), 0.5),
        "c_ctx": nrm(ks[6], (D_MODEL,), 1.0),
        "w_mod": nrm(ks[7], (DEPTH, D_MODEL, 3 * D_MODEL), 0.5 * D_MODEL ** -0.5),
        "b_mod": nrm(ks[8], (DEPTH, 3 * D_MODEL), 0.02),
        "norm_g": 1.0 + nrm(ks[9], (DEPTH, D_MODEL), 0.02),
        "w_in": nrm(ks[10], (DEPTH, D_MODEL, D_IN), D_MODEL ** -0.5),
        "w_gk_f": nrm(ks[11], (DEPTH, GATE_RANK, GLA_K_WIDTH), GATE_RANK ** -0.5),
        "b_gk_f": nrm(ks[12], (DEPTH, GLA_K_WIDTH), 0.1),
        "w_gk_b": nrm(ks[13], (DEPTH, GATE_RANK, GLA_K_WIDTH), GATE_RANK ** -0.5),
        "b_gk_b": nrm(ks[14], (DEPTH, GLA_K_WIDTH), 0.1),
        "sink": nrm(ks[15], (DEPTH, N_ATTN_HEADS), 0.5),
        "gla_norm_g": 1.0 + nrm(ks[16], (DEPTH, GLA_DV), 0.02),
        "w_out": nrm(ks[17], (DEPTH, D_MIX, D_MODEL), D_MIX ** -0.5),
        "final_norm_g": 1.0 + nrm(ks[18], (D_MODEL,), 0.02),
    }


def reference(x_prompt, x_sample, c, cache_k, cache_v, state_gla, c_ctx, w_mod, b_mod, norm_g, w_in,
              w_gk_f, b_gk_f, w_gk_b, b_gk_b, sink, gla_norm_g, w_out, final_norm_g):
    xp = x_prompt
    xs = x_sample
    bp = xp.shape[0]
    new_k_list, new_v_list, new_s_list = [], [], []
    for l in range(DEPTH):
        shift, scale, gate = adaln(c_ctx[None, :], w_mod[l], b_mod[l])
        q, k, v, za, qg, kg, vg, g_f, g_b, zg = project(xp, shift, scale, norm_g[l], w_in[l], w_gk_f[l], b_gk_f[l], w_gk_b[l], b_gk_b[l])
        o_a = context_attention(q, k, v, sink[l])
        s_zero = jnp.zeros((bp, N_GLA_HEADS, GLA_DK, GLA_DV), xp.dtype)
        o_g, s_f, s_b = gla_bidir(qg, kg, vg, g_f, g_b, s_zero, s_zero)
        xp = xp + gate * merge(o_a, za, o_g, zg, gla_norm_g[l], w_out[l])
        new_k_list.append(k)
        new_v_list.append(v)
        new_s_list.append(jnp.stack([s_f, s_b], axis=1))

        shift, scale, gate = adaln(c, w_mod[l], b_mod[l])
        q, k, v, za, qg, kg, vg, g_f, g_b, zg = project(xs, shift, scale, norm_g[l], w_in[l], w_gk_f[l], b_gk_f[l], w_gk_b[l], b_gk_b[l])
        q = axial_rope(q)
        k = axial_rope(k)
        o_a = latent_attention(q, k, v, cache_k[:, l], cache_v[:, l], sink[l])
        o_g, _, _ = gla_bidir(qg, kg, vg, g_f, g_b, state_gla[:, l, 0], state_gla[:, l, 1])
        xs = xs + gate * merge(o_a, za, o_g, zg, gla_norm_g[l], w_out[l])

    y_prompt = rmsnorm(xp, final_norm_g)
    y_sample = rmsnorm(xs, final_norm_g)
    new_k = jnp.stack(new_k_list, axis=1)
    new_v = jnp.stack(new_v_list, axis=1)
    new_state = jnp.stack(new_s_list, axis=1)
    return (y_prompt, y_sample, new_k, new_v, new_state)
```

```python
import contextlib
import numpy as np
import concourse.bass as bass
import concourse.mybir as mybir
from concourse.bass_utils import run_bass_kernel_spmd

F32 = mybir.dt.float32
BF16 = mybir.dt.bfloat16
AF = mybir.ActivationFunctionType
ALU = mybir.AluOpType

D = 1024
NCORES = 8
CTX_PER_CORE = 4
CTX_T = 2
SMP_OWN_T = 16
SMP_ALL_T = 32
EPS = 1e-6

C_VG, C_KG, C_K, C_V, C_LR, C_Q, C_ZA, C_ZG, C_QG = 0, 512, 768, 896, 1024, 1056, 1568, 2080, 2592
WP_COLS = 1056
WIN_COLS = 2848


class Prog:
    SEM_LAT = 240.0

    def __init__(self, nc):
        self.nc = nc
        self.engs = ("pe", "act", "dve", "pool", "sp")
        self.units = []
        self.open_pe = None
        self.W = {}
        self.R = {}
        self.dram = set()

    def _key(self, ap):
        try:
            n = ap.tensor.name
        except Exception:
            return None
        if n in self.dram:
            return None
        return n

    @staticmethod
    def _elems(ap):
        try:
            sh = ap.shape
            n = 1
            for d in sh[1:]:
                n *= int(d)
            return n
        except Exception:
            return 64

    def _deps(self, eng, ins, outs):
        deps = set()
        for a in ins:
            k = self._key(a)
            if k is None:
                continue
            deps.update(self.W.get(k, {}).values())
            if k.startswith("bank"):
                for (e2, u) in self.R.get(k, []):
                    if e2 != eng:
                        deps.add(u)
        for a in outs:
            k = self._key(a)
            if k is None:
                continue
            deps.update(self.W.get(k, {}).values())
            for (e2, u) in self.R.get(k, []):
                deps.add(u)
        return deps

    def _record(self, tag, uid, ins, outs):
        for a in outs:
            k = self._key(a)
            if k is not None:
                if tag in self.engs:
                    self.W.setdefault(k, {})[tag] = uid
                else:
                    d = self.W.setdefault(k, {})
                    for q in range(3, 0, -1):
                        if (tag, q - 1) in d:
                            d[(tag, q)] = d[(tag, q - 1)]
                    d[(tag, 0)] = uid
                self.R[k] = []
        for a in ins:
            k = self._key(a)
            if k is not None:
                self.R.setdefault(k, []).append((tag, uid))

    def op(self, eng, fn, ins, outs, inc=True, cost=None):
        deps = self._deps(eng, ins, outs)
        if cost is None:
            n = self._elems(outs[0]) if outs else 64
            if eng == "act":
                cost = 0.92 * (230.0 + 0.85 * n)
            elif eng == "dve":
                cost = 1.3 * (130.0 + 0.9 * n)
            elif eng == "pool":
                cost = 1.05 * (200.0 + 1.7 * n)
            else:
                cost = 0.70 * (70.0 + 0.55 * max(n, 64))
        if eng == "pe":
            if self.open_pe is None:
                self.units.append(dict(eng="pe", fns=[], deps=set(), cost=0.0, dma_sem=None))
                self.open_pe = len(self.units) - 1
            uid = self.open_pe
            u = self.units[uid]
            u["fns"].append(fn)
            u["deps"].update(d for d in deps if d != uid)
            u["cost"] += cost
            if inc:
                self.open_pe = None
        else:
            self.units.append(dict(eng=eng, fns=[fn], deps=deps, cost=cost, dma_sem=None))
            uid = len(self.units) - 1
        self._record(eng, uid, ins, outs)

    def dma(self, eng, out, in_, slow=False, semkey=None):
        ko, ki = self._key(out), self._key(in_)
        sem = ("dw_" + ko if ko is not None else "dr_" + ki) + "_" + eng
        if semkey is not None:
            sem = semkey + "_" + eng
        deps = self._deps(eng, [in_], [out])
        try:
            nbytes = 1
            for d in out.shape:
                nbytes *= int(d)
            nbytes *= 4
        except Exception:
            nbytes = 65536
        if slow:
            fn = lambda e, o=out, i=in_: e.dma_start(out=o, in_=i, allow_slow_non_contiguous=True)
        else:
            fn = lambda e, o=out, i=in_: e.dma_start(out=o, in_=i)
        self.units.append(dict(eng=eng, fns=[fn], deps=deps, cost=100.0 + nbytes / 200.0, dma_sem=sem, dma_lat=2000.0,
                               final=(ko is None)))
        uid = len(self.units) - 1
        self._record(sem, uid, [in_], [out])

    def mm(self, out, lhsT, rhs, start=True, stop=True, inc=False, sgc=False):
        n = self._elems(out)
        cost = 0.70 * (70.0 + 0.55 * max(n, 64))
        try:
            if lhsT.dtype == F32:
                cost *= 3.5
            elif int(lhsT.shape[0]) == 64 and n >= 512:
                cost *= 1.5
        except Exception:
            pass
        if sgc:
            self.op("pe", lambda e: e.matmul(out, lhsT=lhsT, rhs=rhs, start=start, stop=stop, skip_group_check=True), [lhsT, rhs], [out], inc=inc, cost=cost)
        else:
            self.op("pe", lambda e: e.matmul(out, lhsT=lhsT, rhs=rhs, start=start, stop=stop), [lhsT, rhs], [out], inc=inc, cost=cost)

    def tr(self, out, in_, ident, inc=False):
        self.op("pe", lambda e: e.transpose(out=out, in_=in_, identity=ident), [in_, ident], [out], inc=inc)

    def act(self, out, in_, func, bias=None, scale=None, accum_out=None, eng="act"):
        kw = {}
        ins = [in_]
        outs = [out]
        if bias is not None:
            kw["bias"] = bias
            if not isinstance(bias, (int, float)):
                ins.append(bias)
        if scale is not None:
            kw["scale"] = scale
            if not isinstance(scale, (int, float)):
                ins.append(scale)
        if accum_out is not None:
            kw["accum_out"] = accum_out
            outs.append(accum_out)
        self.op("act", lambda e: e.activation(out=out, in_=in_, func=func, **kw), ins, outs)

    def tt(self, eng, out, in0, in1, op):
        self.op(eng, lambda e: e.tensor_tensor(out=out, in0=in0, in1=in1, op=op), [in0, in1], [out])

    def ts(self, eng, out, in0, s1, op0, s2=None, op1=None):
        ins = [in0] + [s for s in (s1, s2) if s is not None and not isinstance(s, (int, float))]
        cost = None
        k0 = self._key(in0)
        if eng == "dve" and k0 is not None and not k0.startswith("bank"):
            cost = 150.0 + 0.6 * self._elems(out)
        if op1 is None:
            self.op(eng, lambda e: e.tensor_scalar(out=out, in0=in0, scalar1=s1, scalar2=None, op0=op0), ins, [out], cost=cost)
        else:
            self.op(eng, lambda e: e.tensor_scalar(out=out, in0=in0, scalar1=s1, scalar2=s2, op0=op0, op1=op1), ins, [out], cost=cost)

    def stt(self, eng, out, in0, scalar, in1, op0, op1):
        ins = [in0, in1] + ([] if isinstance(scalar, (int, float)) else [scalar])
        self.op(eng, lambda e: e.scalar_tensor_tensor(out=out, in0=in0, scalar=scalar, in1=in1, op0=op0, op1=op1), ins, [out])

    def cp(self, eng, out, in_):
        if eng == "act":
            self.op("act", lambda e: e.activation(out=out, in_=in_, func=AF.Copy), [in_], [out])
        else:
            self.op(eng, lambda e: e.tensor_copy(out=out, in_=in_), [in_], [out])

    def recip(self, out, in_):
        self.op("dve", lambda e: e.reciprocal(out=out, in_=in_), [in_], [out], cost=150.0 + 3.2 * self._elems(out))

    def memset(self, eng, ap, val):
        self.op(eng, lambda e: e.memset(ap, val), [], [ap])

    def finalize(self):
        import heapq
        nc = self.nc
        assert self.open_pe is None, "PE group left open (last PE instruction must have inc=True)"
        U = self.units
        n = len(U)
        succ = [[] for _ in range(n)]
        ndep = [0] * n
        for i, u in enumerate(U):
            u["deps"].discard(i)
            ndep[i] = len(u["deps"])
            for d in u["deps"]:
                succ[d].append(i)
        blevel = [0.0] * n
        for i in range(n - 1, -1, -1):
            u = U[i]
            own = u["cost"] + (u["dma_lat"] if u["dma_sem"] else 0.0)
            m = 0.0
            for j in succ[i]:
                if blevel[j] > m:
                    m = blevel[j]
            blevel[i] = own + self.SEM_LAT + m
        ready_t = [0.0] * n
        fin = [0.0] * n
        free = {e: 0.0 for e in self.engs}
        avail = {e: [] for e in self.engs}
        for i, u in enumerate(U):
            if ndep[i] == 0:
                avail[u["eng"]].append(i)
        order = {e: [] for e in self.engs}
        done = 0
        while done < n:
            best = None
            for e in self.engs:
                if not avail[e]:
                    continue
                t = max(free[e], min(ready_t[i] for i in avail[e]))
                if best is None or t < best[0]:
                    best = (t, e)
            assert best is not None, "scheduler stuck (dependency cycle?)"
            T, e = best
            cand = [i for i in avail[e] if ready_t[i] <= T]
            if PRIO_MODE == 0:
                i = min(cand)
            else:
                i = max(cand, key=lambda c: (blevel[c], -c))
            avail[e].remove(i)
            u = U[i]
            free[e] = T + u["cost"]
            fin[i] = T + (u["cost"] + u["dma_lat"] if u["dma_sem"] else u["cost"])
            order[e].append(i)
            done += 1
            for j in succ[i]:
                ready_t[j] = max(ready_t[j], fin[i] + self.SEM_LAT)
                ndep[j] -= 1
                if ndep[j] == 0:
                    avail[U[j]["eng"]].append(j)
        self.sim_makespan = max(fin) if n else 0.0
        cnt = {}
        semof = [None] * n
        valof = [0] * n
        for e in self.engs:
            for i in order[e]:
                u = U[i]
                sk = u["dma_sem"] if u["dma_sem"] else e
                inc = 16 if u["dma_sem"] else 1
                cnt[sk] = cnt.get(sk, 0) + inc
                semof[i], valof[i] = sk, cnt[sk]
        es = contextlib.ExitStack()
        sems = {k: es.enter_context(nc.semaphore("s_" + k)) for k in cnt}
        streams = {}
        for e in self.engs:
            seen = {}
            st = []
            for i in order[e]:
                u = U[i]
                need = {}
                for d in u["deps"]:
                    sk, v = semof[d], valof[d]
                    if sk == "pe" and e == "pe":
                        continue
                    if v > need.get(sk, 0):
                        need[sk] = v
                for sk, v in need.items():
                    if seen.get(sk, 0) < v:
                        seen[sk] = v
                        st.append(("wait", sk, v))
                nf = len(u["fns"])
                for j, fn in enumerate(u["fns"]):
                    st.append(("op", fn, semof[i] if j == nf - 1 else None, 16 if u["dma_sem"] else 1))
            streams[e] = st
        for i, u in enumerate(U):
            if u["dma_sem"] and u.get("final"):
                pass
        fin_w = {}
        for i, u in enumerate(U):
            if u["dma_sem"] and u.get("final"):
                fin_w[semof[i]] = max(fin_w.get(semof[i], 0), valof[i])
        for sk, v in fin_w.items():
            streams["sp"].append(("wait", sk, v))
        for e in ("pe", "act", "dve", "pool"):
            if cnt.get(e, 0) > 0:
                streams["sp"].append(("wait", e, cnt[e]))
        engobj = {"pe": "tensor", "act": "scalar", "dve": "vector", "pool": "gpsimd", "sp": "sync"}
        with es, nc.Block() as block:
            def run(engname):
                def body(e):
                    for it in streams[engname]:
                        if it[0] == "wait":
                            e.wait_ge(sems[it[1]], it[2])
                        else:
                            ins = it[1](e)
                            if it[2] is not None:
                                ins.then_inc(sems[it[2]], it[3])
                return body
            block.sync(run("sp"))
            block.tensor(run("pe"))
            block.scalar(run("act"))
            block.vector(run("dve"))
            block.gpsimd(run("pool"))


N_THREADS = 2
PRIO_MODE = 1
LAG = 9


def build_program():
    nc = bass.Bass("TRN2", target_bir_lowering=False)
    P = Prog(nc)

    def din(name, shape):
        P.dram.add(name)
        return nc.dram_tensor(name, list(shape), F32, kind="ExternalInput").ap()

    def dout(name, shape):
        P.dram.add(name)
        return nc.dram_tensor(name, list(shape), F32, kind="ExternalOutput").ap()

    xp = din("xp", [CTX_PER_CORE, 256, D])
    xs = din("xs", [4096, D])
    cv_d = din("cv", [2, D])
    ck_d = din("ck", [512, 128])
    cvv_d = din("cvv", [512, 128])
    st_d = din("st", [2, 4, 64, 128])
    wmod_d = din("wmod", [D, 3 * D])
    bmod_d = din("bmod", [3 * D])
    ng_d = din("ng", [D])
    win_d = din("win", [D, WIN_COLS])
    wlrs_d = din("wlrs", [D, 32])
    wgk_d = din("wgk", [2, 33, 512])
    sink_d = din("sink", [8])
    gng_d = din("gng", [128])
    wout_d = din("wout", [D, D])
    fng_d = din("fng", [D])
    ident_d = din("ident", [128, 128])
    tri_d = din("tri", [2, 128, 128])
    chsel_d = din("chsel", [128, 2])
    gmask_d = din("gmask", [2, 128, 128])
    wmask_d = din("wmask", [2, 128, 128])
    rope_d = din("rope", [2, 17 * 128, 64])

    yp_o = dout("yp", [CTX_PER_CORE, 256, D])
    ys_o = dout("ys", [2048, D])
    nk_o = dout("nk", [CTX_PER_CORE, 256, 128])
    nv_o = dout("nv", [CTX_PER_CORE, 256, 128])
    ns_o = dout("ns", [CTX_PER_CORE, 2, 4, 64, 128])

    scr_h = [nc.dram_tensor("scr_h%d" % i, [128, 1024], BF16, kind="Internal").ap() for i in range(24)]
    scr_v = [nc.dram_tensor("scr_v%d" % i, [128, 512], BF16, kind="Internal").ap() for i in range(24)]
    scr_k = [nc.dram_tensor("scr_k%d" % i, [128, 256], F32, kind="Internal").ap() for i in range(24)]

    sb = lambda name, shape, dt=F32: nc.alloc_sbuf_tensor("sb_" + name, list(shape), dt)

    winP = sb("winP", [128, 8, WP_COLS], BF16)
    winM = sb("winM", [128, 8, WIN_COLS - WP_COLS], BF16)

    def wslice(k, c0, c1):
        if c1 <= WP_COLS:
            return winP[:, k, c0:c1]
        assert c0 >= WP_COLS
        return winM[:, k, c0 - WP_COLS:c1 - WP_COLS]
    wlrs = sb("wlrs_sb", [128, 8, 32], BF16)
    wout = sb("wout_sb", [128, 8, D], BF16)
    wgk = [sb("wgk_sb%d" % i, [33, 512], BF16) for i in range(2)]
    ident = sb("ident_sb", [128, 128], BF16)
    tri = [sb("tri_sb%d" % i, [128, 128], F32) for i in range(2)]
    chsel = sb("chsel_sb", [128, 2], F32)
    gmask2 = sb("gmask2", [128, 2, 128], BF16)
    wmask = [sb("wmask_sb%d" % i, [128, 128], BF16) for i in range(2)]
    ropec = sb("ropec", [128, 17, 64], F32)
    ropes = sb("ropes", [128, 17, 64], F32)
    gate_bc = [sb("gate_bc%d" % i, [128, D], F32) for i in range(2)]
    fng_bc = sb("fng_bc", [128, D], F32)
    gng_bc = sb("gng_bc", [128, 128], F32)
    esink = sb("esink", [128, 8], F32)
    gmod = sb("gmod", [128, 8, 2], F32)
    shiftv = sb("shiftv", [128, 8, 2], F32)
    ckT = sb("ckT", [128, 512], BF16)
    cva = sb("cva", [128, 4, 2, 65], BF16)
    kT_st = [sb("kT_st%d" % i, [128, 128], BF16) for i in range(17)]
    va_st = [sb("va_st%d" % i, [128, 2, 65], BF16) for i in range(17)]
    sprev = [sb("sprev%d" % i, [128, 2, 128], BF16) for i in range(32)]
    cvT = sb("cvT", [128, 8, 2], F32)
    cve = sb("cve", [128, 8, 2], F32)
    scT = sb("scT", [128, 8, 2], F32)
    bmT = sb("bmT", [128, 24], F32)
    ngT = sb("ngT", [128, 8], F32)
    modT = sb("modT", [128, 16, 2], F32)
    sink_bc = sb("sink_bc", [128, 8], F32)

    class TR:
        pass

    B = [nc.alloc_psum_tensor("bank%d" % i, [128, 512], F32) for i in range(8)]
    threads = []
    for th in range(N_THREADS):
        R = TR()
        R.th = th
        n = lambda s_: "%s_t%d" % (s_, th)
        R.b = [B[4 * th + i] for i in range(4)] if N_THREADS == 2 else [B[i] for i in range(4)]
        R.bb = [x[:].bitcast(BF16) for x in R.b]
        R.xt = [sb(n("xt%d" % i), [128, D], F32) for i in range(2)]
        R.slot = 0
        R.xn = sb(n("xn"), [128, D], BF16)
        R.hTa = sb(n("hTa"), [128, 4, 128], BF16)
        R.hTb = sb(n("hTb"), [128, 4, 128], BF16)
        R.F = [sb(n("F%d" % i), [128, 512], F32) for i in range(6)]
        R.q_sb = R.e_g = R.oa = R.F[0]
        R.qA = R.spl = R.on = R.F[1]
        R.qB = R.Eg = R.F[2]
        R.e_z = R.Einv = R.F[3]
        R.sz = [R.F[4], R.F[5]]
        R.xg = sb(n("xg"), [128, D], F32)
        R.xr = R.xg
        R.qr = sb(n("qr"), [128, 512], BF16)
        R.qT = sb(n("qT"), [128, 8, 128], BF16)
        R.kr = R.qr[:, 0:128]
        R.vg_bf = sb(n("vg_bf"), [128, 512], BF16)
        R.lrT = sb(n("lrT"), [33, 128], BF16)
        R.qd = sb(n("qd"), [128, 512], BF16)
        R.ki = sb(n("ki"), [128, 512], BF16)
        R.qkT = sb(n("qkT"), [128, 8, 128], BF16)
        R.yT = R.qkT
        R.Am2 = sb(n("Am2"), [128, 2, 4, 128], BF16)
        R.PT = [sb(n("PT%d" % i), [128, 512], BF16) for i in range(3)]
        R.pti = 0
        R.ya = sb(n("ya"), [128, 512], BF16)
        R.kvo = R.F[2][:, 0:256]
        R.stat = sb(n("stat"), [128, 32], F32)
        R.ss, R.lnv, R.rstd = R.stat[:, 0:1], R.stat[:, 1:2], R.stat[:, 2:3]
        R.ss2, R.lnv2, R.rstd2 = R.stat[:, 3:4], R.stat[:, 4:5], R.stat[:, 5:6]
        R.sso, R.lno, R.rso = R.stat[:, 8:12], R.stat[:, 12:16], R.stat[:, 16:20]
        R.den, R.rden = R.stat[:, 20:28], sb(n("rden"), [128, 8], F32)
        R.egt = sb(n("egt"), [128, 8], F32)
        R.kg_sb = sb(n("kg_sb"), [128, 256], F32)
        R.yg = R.kg_sb[:].bitcast(BF16)
        R.S_f = sb(n("S_f"), [128, 2, 128], F32)
        R.S_b = sb(n("S_b"), [128, 2, 128], F32)
        R.Sbf = [sb(n("Sbf%d" % i), [128, 2, 128], BF16) for i in range(2)]
        threads.append(R)
    R0, R1 = threads[0], threads[-1]

    P.dma("pool", ident[:], ident_d)
    for i in range(2):
        P.dma("pool", wgk[i][:], wgk_d[i])
        P.dma("pool", gmask2[:, i, :], gmask_d[i])
        P.dma("pool", wmask[i][:], wmask_d[i])
        P.dma("sp", tri[i][:], tri_d[i])
    P.dma("sp", chsel[:], chsel_d)
    P.dma("sp", ropec[:], rope_d[0].rearrange("(t p) d -> p t d", p=128))
    P.dma("sp", ropes[:], rope_d[1].rearrange("(t p) d -> p t d", p=128))
    P.dma("sp", fng_bc[:], fng_d.partition_broadcast(128))
    P.dma("sp", gng_bc[:], gng_d.partition_broadcast(128))
    P.dma("sp", sink_bc[:], sink_d.partition_broadcast(128))
    for w in range(2):
        P.dma("sp", cvT[:, :, w], cv_d[w].rearrange("(c p) -> p c", p=128), slow=True)
    P.dma("sp", bmT[:], bmod_d.rearrange("(c p) -> p c", p=128), slow=True)
    P.dma("sp", ngT[:], ng_d.rearrange("(c p) -> p c", p=128), slow=True)
    for t in range(4):
        P.dma("pool", cva[:, t, :, 0:64], cvv_d[t * 128:(t + 1) * 128, :].rearrange("p (k d) -> p k d", k=2))

    P.memset("pool", cva[:, :, :, 64:65], 1.0)
    for i in range(17):
        P.memset("pool", va_st[i][:, :, 64:65], 1.0)
    for R in threads:
        P.memset("pool", R.lrT[:], 1.0)
        P.memset("pool", R.qT[:], 0.0)
    P.act(esink[:], sink_bc[:], AF.Exp)

    P.act(cve[:], cvT[:], AF.Exp, scale=-1.0)
    P.ts("dve", cve[:], cve[:], 1.0, ALU.add)
    P.recip(cve[:], cve[:])
    P.tt("dve", scT[:], cvT[:], cve[:], ALU.mult)
    wst1 = [[f[:] for f in R0.F[0:4]], [f[:] for f in R1.F[0:4]], [R0.F[4][:], R0.F[5][:], R1.F[4][:], R1.F[5][:]]]
    MB = R0.b
    for k in range(8):
        bufs = wst1[k % 3]
        for p4 in range(4):
            P.dma("sp", bufs[p4], wmod_d[k * 128:(k + 1) * 128, p4 * 512:(p4 + 1) * 512])
        for n_ in range(16):
            P.mm(MB[1][:, n_ * 2:(n_ + 1) * 2], lhsT=bufs[n_ // 4][:, (n_ % 4) * 128:(n_ % 4 + 1) * 128], rhs=scT[:, k, :],
                 start=(k == 0 and n_ == 0), stop=(k == 7), inc=(n_ == 15), sgc=True)
    f32v = lambda t: t[:].bitcast(F32)
    bgrow = [R0.Am2[:].rearrange("p a h t -> p (a h t)").bitcast(F32)[0:2, :], R1.Am2[:].rearrange("p a h t -> p (a h t)").bitcast(F32)[0:2, :]]
    grow = [R0.qkT[:].rearrange("p a t -> p (a t)").bitcast(F32)[0:2, :], R1.qkT[:].rearrange("p a t -> p (a t)").bitcast(F32)[0:2, :]]
    for j in range(2):
        P.dma("sp", bgrow[j], bmod_d[2 * D + j * 512:2 * D + (j + 1) * 512].partition_broadcast(2))
    sel = R1.ya[:].bitcast(F32)[0:2, 0:256].rearrange("p (w t) -> p w t", w=2)
    P.memset("dve", sel, 0.0)
    P.memset("dve", sel[0:1, 0, :], 1.0)
    P.ts("dve", sel[:, 1, :], sel[:, 0, :], -1.0, ALU.mult, 1.0, ALU.add)
    for k in range(8):
        P.dma("pool", winP[:, k, :], win_d[k * 128:(k + 1) * 128, 0:WP_COLS])
    for k in range(8):
        P.dma("pool", wlrs[:, k, :], wlrs_d[k * 128:(k + 1) * 128, :])
    for k in range(8):
        P.dma("pool", winM[:, k, :], win_d[k * 128:(k + 1) * 128, WP_COLS:WIN_COLS])
    for k in range(8):
        P.dma("pool", wout[:, k, :], wout_d[k * 128:(k + 1) * 128, :])
    P.tt("dve", modT[:], MB[1][:, 0:32].rearrange("p (n w) -> p n w", w=2),
         bass.AP(bmT, 0, [[24, 128], [1, 16], [0, 2]]), ALU.add)
    P.cp("dve", shiftv[:], modT[:, 0:8, :])
    P.ts("dve", modT[:, 8:16, :], modT[:, 8:16, :], 1.0, ALU.add)
    P.tt("dve", gmod[:], modT[:, 8:16, :], bass.AP(ngT, 0, [[8, 128], [1, 8], [0, 2]]), ALU.mult)
    GB = [R1.b[2], R1.b[3]]
    wstG = [R0.xg, R1.xg]
    for k in range(8):
        buf = wstG[k % 2]
        P.dma("sp", buf[:], wmod_d[k * 128:(k + 1) * 128, 2 * D:3 * D])
        for j in range(2):
            P.mm(GB[j][0:2, :], lhsT=scT[:, k, :], rhs=buf[:, j * 512:(j + 1) * 512],
                 start=(k == 0), stop=(k == 7), inc=(j == 1))
    for j in range(2):
        P.tt("dve", grow[j], GB[j][0:2, :], bgrow[j], ALU.add)
    for w in range(2):
        for j in range(2):
            P.mm(GB[j][:, :], lhsT=sel[:, w, :], rhs=grow[j], start=True, stop=True, inc=(j == 1))
        for j in range(2):
            P.cp("act", gate_bc[w][:, j * 512:(j + 1) * 512], GB[j][:, :])
    cktmp = R0.F[4]
    P.dma("sp", cktmp[:, 0:512].rearrange("p (t c) -> p t c", t=4), ck_d.rearrange("(t p) c -> p t c", p=128))
    ck_bf = R0.qr[:].rearrange("p (t c) -> p t c", t=4)
    P.cp("dve", ck_bf, cktmp[:, 0:512].rearrange("p (t c) -> p t c", t=4))
    for t in range(4):
        P.tr(R0.bb[0][:, t * 128:(t + 1) * 128], ck_bf[:, t, :], ident[:], inc=(t == 3))
    P.cp("dve", ckT[:], R0.bb[0][:, 0:512])

    def hT(R, k):
        return (R.hTa if k < 4 else R.hTb)[:, k % 4, :]

    def prep_h(R, which):
        x = R.xt[R.slot]
        P.act(R.xn[:], x[:], AF.Square, accum_out=R.ss)
        P.act(R.lnv, R.ss, AF.Ln, scale=1.0 / D, bias=EPS)
        P.act(R.rstd, R.lnv, AF.Exp, scale=-0.5)
        P.ts("dve", R.xn[:], x[:], R.rstd, ALU.mult)
        for k in range(8):
            P.tr(R.bb[0][:, k * 128:(k + 1) * 128], R.xn[:, k * 128:(k + 1) * 128], ident[:], inc=(k == 7))
        yield
        for k in range(8):
            dst = hT(R, k)
            src = R.bb[0][:, k * 128:(k + 1) * 128]
            if k < 6:
                P.act(dst, src, AF.Identity, scale=gmod[:, k, which:which + 1], bias=shiftv[:, k, which:which + 1])
            else:
                P.ts("dve", dst, src, gmod[:, k, which:which + 1], ALU.mult, shiftv[:, k, which:which + 1], ALU.add)

    def proj(R, bank_ap, c0, ncols):
        for k in range(8):
            P.mm(bank_ap, lhsT=hT(R, k), rhs=wslice(k, c0, c0 + ncols), start=(k == 0), stop=(k == 7), inc=(k == 7))

    def gates(R, which, dirs):
        G = R.b[3]
        for k in range(8):
            lhs = wslice(k, C_LR, C_LR + 32) if which == 0 else wlrs[:, k, :]
            P.mm(G[0:32, 0:128], lhsT=lhs, rhs=hT(R, k), start=(k == 0), stop=(k == 7), inc=(k == 7))
        P.cp("dve", R.lrT[0:32, :], G[0:32, 0:128])
        c0, c1 = (0, 512) if dirs == "fb" else (256, 512)
        P.mm(G[:, c0:c1], lhsT=R.lrT[:, :], rhs=wgk[which][:, c0:c1], start=True, stop=True, inc=True)
        P.act(R.e_g[:, c0:c1], G[:, c0:c1], AF.Exp, scale=-1.0)
        P.act(R.spl[:, c0:c1], R.e_g[:, c0:c1], AF.Ln, bias=1.0)
        if dirs == "fb":
            P.mm(G[:, 0:256], lhsT=tri[0][:], rhs=R.spl[:, 0:256], start=True, stop=True, inc=False)
        P.mm(G[:, 256:512], lhsT=tri[1][:], rhs=R.spl[:, 256:512], start=True, stop=True, inc=True)

    def gtot(R, dirs, bank_ap):
        dl = (0, 1) if dirs == "fb" else ((0,) if dirs == "f" else (1,))
        for d_ in dl:
            for hp in range(2):
                col = (d_ * 2 + hp) * 2
                last = (d_ == dl[-1] and hp == 1)
                P.mm(bank_ap[:, col:col + 2], lhsT=R.spl[:, d_ * 256 + hp * 128:d_ * 256 + (hp + 1) * 128], rhs=chsel[:],
                     start=True, stop=True, inc=last)
        c0, c1 = {"fb": (0, 8), "f": (0, 4), "b": (4, 8)}[dirs]
        P.act(R.egt[:, c0:c1], bank_ap[:, c0:c1], AF.Exp)

    def rope(src_f32, A_, B_, t_idx, nheads, out_ap, a_view=None, b_view=None):
        s3 = src_f32.rearrange("p (h d) -> p h d", d=64)
        cosb = bass.AP(ropec, t_idx * 64, [[17 * 64, 128], [0, nheads], [1, 64]])
        P.tt("pool", A_.rearrange("p (h d) -> p h d", d=64), s3, cosb, ALU.mult)
        base = src_f32.offset
        pitch = src_f32.ap[0][0]
        for half in range(2):
            o_ = bass.AP(B_.tensor, B_.offset + half * 16, [[B_.ap[0][0], 128], [64, nheads], [32, 2], [1, 16]])
            i_ = bass.AP(src_f32.tensor, base + (1 - half) * 16, [[pitch, 128], [64, nheads], [32, 2], [1, 16]])
            s_ = bass.AP(ropes, t_idx * 64 + half * 16, [[17 * 64, 128], [0, nheads], [32, 2], [1, 16]])
            P.tt("pool", o_, i_, s_, ALU.mult)
        P.tt("pool", out_ap, A_ if a_view is None else a_view, B_ if b_view is None else b_view, ALU.add)

    def silu_dve(R, bank, out_buf):
        e = R.e_z
        P.act(e[:], bank[:], AF.Exp, scale=-1.0)
        P.ts("dve", e[:], e[:], 1.0, ALU.add)
        P.recip(e[:], e[:])
        P.tt("dve", out_buf[:], bank[:], e[:], ALU.mult)

    def silu_act(R, bank, out_buf):
        e = R.e_z
        P.act(e[:], bank[:], AF.Exp, scale=-1.0)
        P.act(e[:], e[:], AF.Ln, bias=1.0)
        P.act(e[:], e[:], AF.Exp, scale=-1.0)
        P.tt("dve", out_buf[:], bank[:], e[:], ALU.mult)

    def prepass_tile(R, which, is_smp, t_idx, own, need_kv, sprev_base, kv_slot, S_b, kvout=None, scr_id=None):
        b = R.b
        yield from prep_h(R, which)
        if scr_id is not None:
            sk = "scr%d" % scr_id
            P.dma("sp", scr_h[scr_id][:, 0:512], R.hTa[:].rearrange("p a t -> p (a t)"), semkey="st_hTa_t%d" % R.th)
            P.dma("sp", scr_h[scr_id][:, 512:1024], R.hTb[:].rearrange("p a t -> p (a t)"), semkey="st_hTb_t%d" % R.th)
        proj(R, b[1][:], C_VG, 512)
        yield
        P.cp("act", R.vg_bf[:], b[1][:])
        if scr_id is not None:
            P.dma("sp", scr_v[scr_id], R.vg_bf[:], semkey="st_vg_t%d" % R.th)
        if need_kv:
            proj(R, b[2][:], C_KG, 512)
        else:
            proj(R, b[2][:, 0:256], C_KG, 256)
        if scr_id is not None:
            P.cp("act", R.kg_sb[:], b[2][:, 0:256])
            P.dma("sp", scr_k[scr_id], R.kg_sb[:], semkey="st_kg_t%d" % R.th)
        yield
        gates(R, which, "b")
        yield
        gtot(R, "b", b[1])
        P.act(R.Einv[:, 256:512], b[3][:, 256:512], AF.Exp, scale=-1.0)
        P.tt("dve", R.ki[:, 256:512], b[2][:, 0:256], R.Einv[:, 256:512], ALU.mult)
        yield
        if need_kv:
            if kvout is not None:
                P.cp("act", R.kvo, b[2][:, 256:512])
                P.dma("sp", kvout[0], R.kvo[:, 0:128])
                P.dma("sp", kvout[1], R.kvo[:, 128:256])
            if is_smp:
                ks = R.q_sb[:, 0:128]
                P.cp("act", ks, b[2][:, 256:384])
                rope(ks, R.qA[:, 0:128], R.qB[:, 0:128], t_idx, 2, R.kr)
            else:
                P.cp("act", R.kr, b[2][:, 256:384])
            P.cp("dve", va_st[kv_slot][:, :, 0:64], b[2][:, 384:512].rearrange("p (k d) -> p k d", k=2))
            P.tr(R.bb[0][:, 0:128], R.kr, ident[:], inc=True)
            P.cp("dve", kT_st[kv_slot][:], R.bb[0][:, 0:128])
        yield
        DBs = [b[3], b[1]]
        for c in (1, 0):
            DB = DBs[c]
            if own:
                P.cp("pool", sprev[sprev_base + c][:], S_b[:])
            for h in range(4):
                hp, hh = h // 2, h % 2
                P.mm(DB[hh * 64:(hh + 1) * 64, hp * 128:(hp + 1) * 128],
                     lhsT=R.ki[c * 64:(c + 1) * 64, 256 + h * 64:256 + (h + 1) * 64],
                     rhs=R.vg_bf[c * 64:(c + 1) * 64, h * 128:(h + 1) * 128], start=True, stop=True, inc=(h == 3))
            for hp in range(2):
                col = (2 + hp) * 2 + c
                P.tt("dve", S_b[:, hp, :], S_b[:, hp, :], DB[:, hp * 128:(hp + 1) * 128], ALU.add)
                P.ts("dve", S_b[:, hp, :], S_b[:, hp, :], R.egt[:, col:col + 1], ALU.mult)
            if c == 1:
                yield

    def main_tile(R, which, is_smp, t_idx, sprev_base, key_list, y_dst, S_f, scr_id):
        b, bb = R.b, R.bb
        x = R.xt[R.slot]
        P.dma("sp", R.hTa[:].rearrange("p a t -> p (a t)"), scr_h[scr_id][:, 0:512])
        P.dma("sp", R.hTb[:].rearrange("p a t -> p (a t)"), scr_h[scr_id][:, 512:1024])
        P.dma("sp", R.vg_bf[:], scr_v[scr_id])
        P.dma("sp", R.kg_sb[:], scr_k[scr_id])
        yield
        proj(R, b[1][:], C_Q, 512)
        yield
        qperm = bass.AP(R.qr, 0, [[512, 128], [64, 2], [128, 4], [1, 64]])
        if is_smp:
            P.cp("act", R.q_sb[:], b[1][:])
            rope(R.q_sb[:], R.qA[:], R.qB[:], t_idx, 8, qperm,
                 a_view=R.qA[:].rearrange("p (k g d) -> p k g d", k=2, g=4),
                 b_view=R.qB[:].rearrange("p (k g d) -> p k g d", k=2, g=4))
        else:
            P.cp("act", qperm, b[1][:].rearrange("p (k g d) -> p k g d", k=2, g=4))
        proj(R, b[2][:], C_ZA, 512)
        yield
        silu_act(R, b[2], R.sz[0])
        proj(R, b[1][:], C_ZG, 512)
        yield
        silu_act(R, b[1], R.sz[1])
        P.tt("pool", R.sz[1][:].rearrange("p (h v) -> p h v", h=4), R.sz[1][:].rearrange("p (h v) -> p h v", h=4),
             bass.AP(gng_bc, 0, [[128, 128], [0, 4], [1, 128]]), ALU.mult)
        yield
        proj(R, b[1][:, 0:256], C_QG, 256)
        for g in range(4):
            P.tr(bb[0][:, g * 128:(g + 1) * 128], R.qr[:, g * 128:(g + 1) * 128], ident[:], inc=(g == 3))
        yield
        P.cp("dve", R.qT[0:64, 0:4, :], bb[0][0:64, 0:512].rearrange("p (g t) -> p g t", g=4))
        P.cp("dve", R.qT[64:128, 4:8, :], bb[0][64:128, 0:512].rearrange("p (g t) -> p g t", g=4))
        gates(R, which, "fb")
        yield
        gtot(R, "f", b[2])
        P.act(R.Eg[:], b[3][:], AF.Exp)
        P.act(R.Einv[:], b[3][:], AF.Exp, scale=-1.0)
        qg_b = bass.AP(b[1], 0, [[512, 128], [0, 2], [1, 256]])
        kg_b = bass.AP(R.kg_sb, 0, [[256, 128], [0, 2], [1, 256]])
        P.stt("dve", R.qd[:].rearrange("p (a c) -> p a c", a=2), qg_b, 0.125, R.Eg[:].rearrange("p (a c) -> p a c", a=2), ALU.mult, ALU.mult)
        P.tt("dve", R.ki[:].rearrange("p (a c) -> p a c", a=2), kg_b, R.Einv[:].rearrange("p (a c) -> p a c", a=2), ALU.mult)
        yield
        for i in range(4):
            P.tr(bb[0][:, i * 128:(i + 1) * 128], R.qd[:, i * 128:(i + 1) * 128], ident[:], inc=False)
        for i in range(4):
            P.tr(bb[0][:, (4 + i) * 128:(5 + i) * 128], R.ki[:, i * 128:(i + 1) * 128], ident[:], inc=(i == 3))
        yield
        P.cp("act", R.qkT[:, 0:4, :], bb[0][:, 0:512].rearrange("p (i t) -> p i t", i=4))
        P.cp("act", R.qkT[:, 4:8, :], bb[0][:, 512:1024].rearrange("p (i t) -> p i t", i=4))
        AB = [b[1], b[2]]
        for hh in range(2):
            for d_ in range(2):
                for hp in range(2):
                    last = (d_ == 1 and hp == 1)
                    P.mm(AB[hh][:, (d_ * 2 + hp) * 128:(d_ * 2 + hp + 1) * 128], lhsT=R.qkT[hh * 64:(hh + 1) * 64, 4 + d_ * 2 + hp, :],
                         rhs=R.qkT[hh * 64:(hh + 1) * 64, d_ * 2 + hp, :], start=True, stop=True, inc=last)
        yield
        for hh in range(2):
            P.tt("dve", bass.AP(R.Am2, hh * 128, [[1024, 128], [512, 2], [256, 2], [1, 128]]),
                 AB[hh][:].rearrange("p (a c t) -> p a c t", a=2, c=2),
                 bass.AP(gmask2, 0, [[256, 128], [128, 2], [0, 2], [1, 128]]), ALU.mult)
        OBs = [b[1], b[2]]
        P.cp("pool", R.Sbf[0][:], S_f[:])
        for h in range(4):
            hp, hh = h // 2, h % 2
            OB = OBs[hh]
            oh = OB[:, hp * 128:(hp + 1) * 128]
            P.mm(oh, lhsT=R.Am2[:, 0, h, :], rhs=R.vg_bf[:, h * 128:(h + 1) * 128], start=(hp == 0), stop=False, sgc=True)
            P.mm(oh, lhsT=R.Am2[:, 1, h, :], rhs=R.vg_bf[:, h * 128:(h + 1) * 128], start=False, stop=False, sgc=True)
            for c in range(2):
                P.mm(OB[c * 64:(c + 1) * 64, hp * 128:(hp + 1) * 128],
                     lhsT=R.qkT[hh * 64:(hh + 1) * 64, 2 + hp, c * 64:(c + 1) * 64],
                     rhs=sprev[sprev_base + c][hh * 64:(hh + 1) * 64, hp, :], start=False, stop=False, sgc=True)
            P.mm(OB[0:64, hp * 128:(hp + 1) * 128], lhsT=R.qkT[hh * 64:(hh + 1) * 64, hp, 0:64],
                 rhs=R.Sbf[0][hh * 64:(hh + 1) * 64, hp, :], start=False, stop=False, inc=(h == 3), sgc=True)
        yield
        DSB = [b[3], b[0]]
        for c in range(2):
            for h in range(4):
                hp, hh = h // 2, h % 2
                P.mm(DSB[c][hh * 64:(hh + 1) * 64, hp * 128:(hp + 1) * 128],
                     lhsT=R.ki[c * 64:(c + 1) * 64, h * 64:(h + 1) * 64],
                     rhs=R.vg_bf[c * 64:(c + 1) * 64, h * 128:(h + 1) * 128], start=True, stop=True, inc=(h == 3))
            for hp in range(2):
                col = hp * 2 + c
                P.tt("dve", S_f[:, hp, :], S_f[:, hp, :], DSB[c][:, hp * 128:(hp + 1) * 128], ALU.add)
                P.ts("dve", S_f[:, hp, :], S_f[:, hp, :], R.egt[:, col:col + 1], ALU.mult)
            if c == 0:
                P.cp("pool", R.Sbf[1][:], S_f[:])
                for h in range(4):
                    hp, hh = h // 2, h % 2
                    P.mm(OBs[hh][64:128, hp * 128:(hp + 1) * 128], lhsT=R.qkT[hh * 64:(hh + 1) * 64, hp, 64:128],
                         rhs=R.Sbf[1][hh * 64:(hh + 1) * 64, hp, :], start=False, stop=True, inc=(h == 3), sgc=True)
        yield
        for h in range(4):
            hp, hh = h // 2, h % 2
            P.act(R.xn[:, h * 128:(h + 1) * 128], OBs[hh][:, hp * 128:(hp + 1) * 128], AF.Square, accum_out=R.sso[:, h:h + 1])
        P.act(R.lno, R.sso, AF.Ln, scale=1.0 / 128.0, bias=EPS)
        P.act(R.rso, R.lno, AF.Exp, scale=-0.5)
        for hh in range(2):
            P.tt("dve", bass.AP(R.on, hh * 128, [[512, 128], [256, 2], [1, 128]]),
                 OBs[hh][:, 0:256].rearrange("p (a v) -> p a v", a=2),
                 bass.AP(R.stat, 16 + hh, [[32, 128], [2, 2], [0, 128]]), ALU.mult)
        P.tt("pool", R.yg[:], R.on[:], R.sz[1][:], ALU.mult)
        yield
        nk = len(key_list)
        SB = [b[0], b[3]]
        items = [(kv, i) for kv in range(2) for i in range(nk)]

        def issue_scores(n_):
            kv, i = items[n_]
            kT_ap = key_list[i][0]
            P.mm(SB[n_ % 2][:], lhsT=kT_ap[:, :], rhs=R.qT[:, kv * 4:(kv + 1) * 4, :],
                 start=True, stop=True, inc=True)

        issue_scores(0)
        for n_, (kv, i) in enumerate(items):
            OA = b[1 + kv]
            kT_ap, va_ap, mi = key_list[i]
            if n_ + 1 < len(items):
                issue_scores(n_ + 1)
            pt = R.PT[R.pti]
            R.pti = (R.pti + 1) % 3
            P.act(pt[:], SB[n_ % 2][:], AF.Exp, scale=0.125)
            if mi is not None:
                P.tt("pool", pt[:].rearrange("p (g t) -> p g t", g=4), pt[:].rearrange("p (g t) -> p g t", g=4),
                     bass.AP(wmask[mi], 0, [[128, 128], [0, 4], [1, 128]]), ALU.mult)
            for g in range(4):
                P.mm(OA[:, g * 65:(g + 1) * 65], lhsT=pt[:, g * 128:(g + 1) * 128], rhs=va_ap[:, kv, :],
                     start=(i == 0 and g == 0), stop=(i == nk - 1), inc=(g == 3), sgc=True)
            if (nk > 3 and i == 3) or (nk <= 3 and i == nk - 1):
                yield
            if i == nk - 1:
                o3 = OA[:, 0:260].rearrange("p (g e) -> p g e", g=4)
                P.tt("dve", R.den[:, kv * 4:(kv + 1) * 4], o3[:, :, 64], esink[:, kv * 4:(kv + 1) * 4], ALU.add)
                P.recip(R.rden[:, kv * 4:(kv + 1) * 4], R.den[:, kv * 4:(kv + 1) * 4])
                P.tt("dve", R.oa[:, kv * 256:(kv + 1) * 256].rearrange("p (g d) -> p g d", g=4), o3[:, :, 0:64],
                     bass.AP(R.rden, kv * 4, [[8, 128], [1, 4], [0, 64]]), ALU.mult)
                yield
        P.tt("pool", R.ya[:], R.oa[:], R.sz[0][:], ALU.mult)
        for i in range(4):
            P.tr(bb[0][:, i * 128:(i + 1) * 128], R.ya[:, i * 128:(i + 1) * 128], ident[:], inc=False)
        for i in range(4):
            P.tr(bb[0][:, (4 + i) * 128:(5 + i) * 128], R.yg[:, i * 128:(i + 1) * 128], ident[:], inc=(i == 3))
        yield
        P.cp("act", R.yT[:, 0:4, :], bb[0][:, 0:512].rearrange("p (i t) -> p i t", i=4))
        P.cp("dve", R.yT[:, 4:8, :], bb[0][:, 512:1024].rearrange("p (i t) -> p i t", i=4))
        for j in range(2):
            for k in range(8):
                P.mm(b[1 + j][:], lhsT=R.yT[:, k, :], rhs=wout[:, k, j * 512:(j + 1) * 512], start=(k == 0), stop=(k == 7), inc=(k == 7))
        yield
        for j in range(2):
            P.tt("dve", R.xg[:, j * 512:(j + 1) * 512], b[1 + j][:], gate_bc[which][:, j * 512:(j + 1) * 512], ALU.mult)
        P.tt("pool", R.xr[:], R.xg[:], x[:], ALU.add)
        P.act(R.xn[:], R.xr[:], AF.Square, accum_out=R.ss2)
        P.act(R.lnv2, R.ss2, AF.Ln, scale=1.0 / D, bias=EPS)
        P.act(R.rstd2, R.lnv2, AF.Exp, scale=-0.5)
        P.stt("dve", R.xg[:], R.xr[:], R.rstd2, fng_bc[:], ALU.mult, ALU.mult)
        P.dma("sp", y_dst, R.xg[:])

    def st_view(ap3):
        return ap3.rearrange("(hp hh) d v -> (hh d) hp v", hh=2)

    def thread_gen(R, jobs):
        if not jobs:
            return
        R.slot = 0
        P.dma("sp", R.xt[0][:], jobs[0][0])
        for ji, (x_ap, pre, fac, post) in enumerate(jobs):
            if pre is not None:
                pre()
            if ji + 1 < len(jobs):
                P.dma("sp", R.xt[1 - R.slot][:], jobs[ji + 1][0])
            yield from fac()
            if post is not None:
                post()
            R.slot = 1 - R.slot
            yield

    def run_threads(gens, lag):
        active = []
        pending = list(gens)
        step = 0
        while active or pending:
            if pending and (not active or step >= lag * (len(gens) - len(pending))):
                active.append(pending.pop(0))
            for g in list(active):
                try:
                    next(g)
                except StopIteration:
                    active.remove(g)
            step += 1

    def ctx_jobs(R, seqs):
        pre_jobs, main_jobs = [], []
        th = R.th
        for j, s_ in enumerate(seqs):
            kslots = (4 * th + 2 * j, 4 * th + 2 * j + 1)
            spb = 8 * th + 4 * j
            keys_ctx = [(kT_st[kslots[0]], va_st[kslots[0]], None), (kT_st[kslots[1]], va_st[kslots[1]], None)]

            def pre_begin(R=R):
                P.memset("pool", R.S_b[:], 0.0)

            def main_begin(R=R):
                P.memset("pool", R.S_f[:], 0.0)
            pre_jobs.append((xp[s_, 128:256, :], pre_begin,
                             lambda R=R, s_=s_, spb=spb, kslots=kslots: prepass_tile(R, 0, False, 0, True, True, spb + 2, kslots[1], R.S_b,
                                                                       kvout=(nk_o[s_, 128:256, :], nv_o[s_, 128:256, :]), scr_id=2 * s_ + 1), None))
            pre_jobs.append((xp[s_, 0:128, :], None,
                             lambda R=R, s_=s_, spb=spb, kslots=kslots: prepass_tile(R, 0, False, 0, True, True, spb, kslots[0], R.S_b,
                                                                       kvout=(nk_o[s_, 0:128, :], nv_o[s_, 0:128, :]), scr_id=2 * s_),
                             lambda R=R, s_=s_: P.dma("sp", st_view(ns_o[s_, 1]), R.S_b[:])))
            main_jobs.append((xp[s_, 0:128, :], main_begin,
                              lambda R=R, s_=s_, spb=spb, keys_ctx=keys_ctx: main_tile(R, 0, False, 0, spb, keys_ctx, yp_o[s_, 0:128, :], R.S_f, 2 * s_), None))
            main_jobs.append((xp[s_, 128:256, :], None,
                              lambda R=R, s_=s_, spb=spb, keys_ctx=keys_ctx: main_tile(R, 0, False, 0, spb + 2, keys_ctx, yp_o[s_, 128:256, :], R.S_f, 2 * s_ + 1),
                              lambda R=R, s_=s_: P.dma("sp", st_view(ns_o[s_, 0]), R.S_f[:])))
        return pre_jobs + main_jobs

    nT = len(threads)
    seq_split = [list(range(CTX_PER_CORE))[i::nT] for i in range(nT)]
    if PHASES >= 1:
        run_threads([thread_gen(threads[i], ctx_jobs(threads[i], seq_split[i])) for i in range(nT)], LAG)

    S_f, S_b = R0.S_f, R0.S_b
    if PHASES >= 2:
        P.dma("sp", S_f[:], st_view(st_d[0]))
        P.dma("sp", S_b[:], st_view(st_d[1]))
        pj = [[] for _ in range(nT)]
        for n_, t in enumerate(range(31, -1, -1)):
            R = threads[n_ % nT]
            own = t < 16
            pj[n_ % nT].append((xs[t * 128:(t + 1) * 128, :], None,
                                lambda R=R, t=t, own=own: prepass_tile(R, 1, True, min(t, 16), own, t <= 16, (2 * t if own else 0),
                                                                       min(t, 16), S_b, scr_id=(8 + t if own else None)), None))
        run_threads([thread_gen(threads[i], pj[i]) for i in range(nT)], 3)

    if PHASES >= 3:
        mj = [[] for _ in range(nT)]
        for t in range(16):
            R = threads[t % nT]
            keys = []
            if t > 0:
                keys.append((kT_st[t - 1], va_st[t - 1], 0))
            keys.append((kT_st[t], va_st[t], None))
            keys.append((kT_st[t + 1], va_st[t + 1], 1))
            for c4 in range(4):
                keys.append((ckT[:, c4 * 128:(c4 + 1) * 128], cva[:, c4, :, :], None))
            mj[t % nT].append((xs[t * 128:(t + 1) * 128, :], None,
                               lambda R=R, t=t, keys=keys: main_tile(R, 1, True, t, 2 * t, keys, ys_o[t * 128:(t + 1) * 128, :], S_f, 8 + t), None))
        run_threads([thread_gen(threads[i], mj[i]) for i in range(nT)], LAG)

    P.finalize()
    return nc


PHASES = 3

_CACHE = {}


def _consts():
    j = np.arange(128)[:, None]
    i = np.arange(128)[None, :]
    same = (j // 64) == (i // 64)
    tri = np.stack([(same & (j <= i)), (same & (j >= i))]).astype(np.float32) * np.float32(-1.0 / 16.0)
    chsel = np.zeros((128, 2), np.float32)
    chsel[:64, 0] = -1.0 / 16.0
    chsel[64:, 1] = -1.0 / 16.0
    gmask = np.stack([(same & (j <= i)), (same & (j >= i))]).astype(np.float32)
    wmask = np.stack([(j >= i), (j <= i)]).astype(np.float32)
    ident = np.eye(128, dtype=np.float32)
    return tri, chsel, gmask, wmask, ident


def _rope_tables(pos):
    row = (pos // 64).astype(np.float64)
    col = (pos % 64).astype(np.float64)
    nf = 16
    freqs = (10000.0 ** (-np.arange(nf, dtype=np.float32) / np.float32(nf))).astype(np.float32).astype(np.float64)
    ar = (row[:, None].astype(np.float32) * freqs[None, :].astype(np.float32)).astype(np.float64)
    ac = (col[:, None].astype(np.float32) * freqs[None, :].astype(np.float32)).astype(np.float64)
    cos = np.concatenate([np.cos(ar), np.cos(ar), np.cos(ac), np.cos(ac)], 1)
    sin = np.concatenate([-np.sin(ar), np.sin(ar), -np.sin(ac), np.sin(ac)], 1)
    return np.stack([cos, sin]).astype(np.float32)


def kernel(x_prompt, x_sample, c, cache_k, cache_v, state_gla, c_ctx, w_mod, b_mod, norm_g, w_in,
           w_gk_f, b_gk_f, w_gk_b, b_gk_b, sink, gla_norm_g, w_out, final_norm_g):
    f = lambda a: np.ascontiguousarray(np.asarray(a, dtype=np.float32))
    x_prompt, x_sample, c, cache_k, cache_v, state_gla = map(f, (x_prompt, x_sample, c, cache_k, cache_v, state_gla))
    c_ctx, w_mod, b_mod, norm_g, w_in = map(f, (c_ctx, w_mod, b_mod, norm_g, w_in))
    w_gk_f, b_gk_f, w_gk_b, b_gk_b, sink, gla_norm_g, w_out, final_norm_g = map(
        f, (w_gk_f, b_gk_f, w_gk_b, b_gk_b, sink, gla_norm_g, w_out, final_norm_g))

    if "nc" not in _CACHE:
        _CACHE["nc"] = build_program()
    nc = _CACHE["nc"]

    wi = w_in[0]
    o_q, o_k, o_v, o_za, o_qg, o_kg, o_vg, o_lrf, o_lrb, o_zg = 0, 512, 640, 768, 1280, 1536, 1792, 2304, 2320, 2336
    win_p = np.concatenate([wi[:, o_vg:o_vg + 512], wi[:, o_kg:o_kg + 256], wi[:, o_k:o_k + 128], wi[:, o_v:o_v + 128],
                            wi[:, o_lrf:o_lrf + 16], wi[:, o_lrb:o_lrb + 16],
                            wi[:, o_q:o_q + 512], wi[:, o_za:o_za + 512], wi[:, o_zg:o_zg + 512], wi[:, o_qg:o_qg + 256]], axis=1)
    win_p = np.ascontiguousarray(win_p)

    def wgk_aug(wf, bf_, wb, bb_):
        m = np.zeros((33, 512), np.float32)
        m[0:16, 0:256] = wf
        m[16:32, 256:512] = wb
        m[32, 0:256] = bf_
        m[32, 256:512] = bb_
        return m

    wgk_ctx = wgk_aug(w_gk_f[0], b_gk_f[0], w_gk_b[0], b_gk_b[0])
    wgk_swp = wgk_aug(w_gk_b[0], b_gk_b[0], w_gk_f[0], b_gk_f[0])
    tri, chsel, gmask, wmask, ident = _consts()

    in_maps = []
    for core in range(NCORES):
        b, h = core // 2, core % 2
        if h == 0:
            xs_l = x_sample[b]
            pos = np.arange(17 * 128)
            wlrs = np.concatenate([wi[:, o_lrf:o_lrf + 16], wi[:, o_lrb:o_lrb + 16]], 1)
            wgk_s = wgk_ctx
            st = state_gla[b, 0]
        else:
            xs_l = x_sample[b, ::-1]
            pos = 4095 - np.arange(17 * 128)
            wlrs = np.concatenate([wi[:, o_lrb:o_lrb + 16], wi[:, o_lrf:o_lrf + 16]], 1)
            wgk_s = wgk_swp
            st = state_gla[b, 0, ::-1]
        in_maps.append({
            "xp": np.ascontiguousarray(x_prompt[core * 4:(core + 1) * 4]),
            "xs": np.ascontiguousarray(xs_l),
            "cv": np.ascontiguousarray(np.stack([c_ctx, c[b]])),
            "ck": np.ascontiguousarray(cache_k[b, 0].reshape(512, 128)),
            "cvv": np.ascontiguousarray(cache_v[b, 0].reshape(512, 128)),
            "st": np.ascontiguousarray(st),
            "wmod": w_mod[0], "bmod": b_mod[0], "ng": norm_g[0],
            "win": win_p, "wlrs": np.ascontiguousarray(wlrs),
            "wgk": np.ascontiguousarray(np.stack([wgk_ctx, wgk_s])),
            "sink": sink[0], "gng": gla_norm_g[0], "wout": w_out[0], "fng": final_norm_g,
            "ident": ident, "tri": tri, "chsel": chsel, "gmask": gmask, "wmask": wmask,
            "rope": _rope_tables(pos),
        })
    if _CACHE.get("sim_hook") is not None:
        return _CACHE["sim_hook"](nc, in_maps)
    res = run_bass_kernel_spmd(nc, in_maps, core_ids=list(range(NCORES)))
    R = res.results
    y_prompt = np.concatenate([R[i]["yp"] for i in range(NCORES)], 0)
    new_k = np.concatenate([R[i]["nk"] for i in range(NCORES)], 0).reshape(32, 1, 256, 2, 64)
    new_v = np.concatenate([R[i]["nv"] for i in range(NCORES)], 0).reshape(32, 1, 256, 2, 64)
    new_state = np.concatenate([R[i]["ns"] for i in range(NCORES)], 0).reshape(32, 1, 2, 4, 64, 128)
    y_sample = np.empty((4, 4096, D), np.float32)
    for core in range(NCORES):
        b, h = core // 2, core % 2
        if h == 0:
            y_sample[b, 0:2048] = R[core]["ys"]
        else:
            y_sample[b, 2048:4096] = R[core]["ys"][::-1]
    return (y_prompt, y_sample, new_k, new_v, new_state)
```

```python
import contextlib
import numpy as np
import concourse.bass as bass
import concourse.mybir as mybir
from concourse.bass_utils import run_bass_kernel_spmd

F32 = mybir.dt.float32
BF16 = mybir.dt.bfloat16
AF = mybir.ActivationFunctionType
ALU = mybir.AluOpType

D = 1024
NCORES = 8
CTX_PER_CORE = 4
CTX_T = 2
SMP_OWN_T = 16
SMP_ALL_T = 32
EPS = 1e-6

C_VG, C_KG, C_K, C_V, C_LR, C_Q, C_ZA, C_ZG, C_QG = 0, 512, 768, 896, 1024, 1056, 1568, 2080, 2592
WP_COLS = 1056
WIN_COLS = 2848


class Prog:
    SEM_LAT = 240.0

    def __init__(self, nc):
        self.nc = nc
        self.engs = ("pe", "act", "dve", "pool", "sp")
        self.units = []
        self.open_pe = None
        self.W = {}
        self.R = {}
        self.dram = set()

    def _key(self, ap):
        try:
            n = ap.tensor.name
        except Exception:
            return None
        if n in self.dram:
            return None
        return n

    @staticmethod
    def _elems(ap):
        try:
            sh = ap.shape
            n = 1
            for d in sh[1:]:
                n *= int(d)
            return n
        except Exception:
            return 64

    def _deps(self, eng, ins, outs):
        deps = set()
        for a in ins:
            k = self._key(a)
            if k is None:
                continue
            deps.update(self.W.get(k, {}).values())
            if k.startswith("bank"):
                for (e2, u) in self.R.get(k, []):
                    if e2 != eng:
                        deps.add(u)
        for a in outs:
            k = self._key(a)
            if k is None:
                continue
            deps.update(self.W.get(k, {}).values())
            for (e2, u) in self.R.get(k, []):
                deps.add(u)
        return deps

    def _record(self, tag, uid, ins, outs):
        for a in outs:
            k = self._key(a)
            if k is not None:
                if tag in self.engs:
                    self.W.setdefault(k, {})[tag] = uid
                else:
                    d = self.W.setdefault(k, {})
                    for q in range(3, 0, -1):
                        if (tag, q - 1) in d:
                            d[(tag, q)] = d[(tag, q - 1)]
                    d[(tag, 0)] = uid
                self.R[k] = []
        for a in ins:
            k = self._key(a)
            if k is not None:
                self.R.setdefault(k, []).append((tag, uid))

    def op(self, eng, fn, ins, outs, inc=True, cost=None):
        deps = self._deps(eng, ins, outs)
        if cost is None:
            n = self._elems(outs[0]) if outs else 64
            if eng == "act":
                cost = 0.92 * (230.0 + 0.85 * n)
            elif eng == "dve":
                cost = 1.3 * (130.0 + 0.9 * n)
            elif eng == "pool":
                cost = 1.05 * (200.0 + 1.7 * n)
            else:
                cost = 0.74 * (70.0 + 0.55 * max(n, 64))
        if eng == "pe":
            if self.open_pe is None:
                self.units.append(dict(eng="pe", fns=[], deps=set(), cost=0.0, dma_sem=None))
                self.open_pe = len(self.units) - 1
            uid = self.open_pe
            u = self.units[uid]
            u["fns"].append(fn)
            u["deps"].update(d for d in deps if d != uid)
            u["cost"] += cost
            if inc:
                self.open_pe = None
        else:
            self.units.append(dict(eng=eng, fns=[fn], deps=deps, cost=cost, dma_sem=None))
            uid = len(self.units) - 1
        self._record(eng, uid, ins, outs)

    def dma(self, eng, out, in_, slow=False, semkey=None):
        ko, ki = self._key(out), self._key(in_)
        sem = ("dw_" + ko if ko is not None else "dr_" + ki) + "_" + eng
        if semkey is not None:
            sem = semkey + "_" + eng
        deps = self._deps(eng, [in_], [out])
        try:
            nbytes = 1
            for d in out.shape:
                nbytes *= int(d)
            nbytes *= 4
        except Exception:
            nbytes = 65536
        if slow:
            fn = lambda e, o=out, i=in_: e.dma_start(out=o, in_=i, allow_slow_non_contiguous=True)
        else:
            fn = lambda e, o=out, i=in_: e.dma_start(out=o, in_=i)
        self.units.append(dict(eng=eng, fns=[fn], deps=deps, cost=100.0 + nbytes / 200.0, dma_sem=sem, dma_lat=1500.0,
                               final=(ko is None)))
        uid = len(self.units) - 1
        self._record(sem, uid, [in_], [out])

    def mm(self, out, lhsT, rhs, start=True, stop=True, inc=False, sgc=False):
        n = self._elems(out)
        cost = 0.74 * (70.0 + 0.55 * max(n, 64))
        try:
            if lhsT.dtype == F32:
                cost *= 3.5
            elif int(lhsT.shape[0]) == 64 and n >= 512:
                cost *= 1.5
        except Exception:
            pass
        if sgc:
            self.op("pe", lambda e: e.matmul(out, lhsT=lhsT, rhs=rhs, start=start, stop=stop, skip_group_check=True), [lhsT, rhs], [out], inc=inc, cost=cost)
        else:
            self.op("pe", lambda e: e.matmul(out, lhsT=lhsT, rhs=rhs, start=start, stop=stop), [lhsT, rhs], [out], inc=inc, cost=cost)

    def tr(self, out, in_, ident, inc=False):
        self.op("pe", lambda e: e.transpose(out=out, in_=in_, identity=ident), [in_, ident], [out], inc=inc)

    def act(self, out, in_, func, bias=None, scale=None, accum_out=None, eng="act"):
        kw = {}
        ins = [in_]
        outs = [out]
        if bias is not None:
            kw["bias"] = bias
            if not isinstance(bias, (int, float)):
                ins.append(bias)
        if scale is not None:
            kw["scale"] = scale
            if not isinstance(scale, (int, float)):
                ins.append(scale)
        if accum_out is not None:
            kw["accum_out"] = accum_out
            outs.append(accum_out)
        self.op("act", lambda e: e.activation(out=out, in_=in_, func=func, **kw), ins, outs)

    def tt(self, eng, out, in0, in1, op):
        self.op(eng, lambda e: e.tensor_tensor(out=out, in0=in0, in1=in1, op=op), [in0, in1], [out])

    def ts(self, eng, out, in0, s1, op0, s2=None, op1=None):
        ins = [in0] + [s for s in (s1, s2) if s is not None and not isinstance(s, (int, float))]
        cost = None
        k0 = self._key(in0)
        if eng == "dve" and k0 is not None and not k0.startswith("bank"):
            cost = 150.0 + 0.6 * self._elems(out)
        if op1 is None:
            self.op(eng, lambda e: e.tensor_scalar(out=out, in0=in0, scalar1=s1, scalar2=None, op0=op0), ins, [out], cost=cost)
        else:
            self.op(eng, lambda e: e.tensor_scalar(out=out, in0=in0, scalar1=s1, scalar2=s2, op0=op0, op1=op1), ins, [out], cost=cost)

    def stt(self, eng, out, in0, scalar, in1, op0, op1):
        ins = [in0, in1] + ([] if isinstance(scalar, (int, float)) else [scalar])
        self.op(eng, lambda e: e.scalar_tensor_tensor(out=out, in0=in0, scalar=scalar, in1=in1, op0=op0, op1=op1), ins, [out])

    def cp(self, eng, out, in_):
        if eng == "act":
            self.op("act", lambda e: e.activation(out=out, in_=in_, func=AF.Copy), [in_], [out])
        else:
            self.op(eng, lambda e: e.tensor_copy(out=out, in_=in_), [in_], [out])

    def recip(self, out, in_):
        self.op("dve", lambda e: e.reciprocal(out=out, in_=in_), [in_], [out], cost=150.0 + 3.2 * self._elems(out))

    def memset(self, eng, ap, val):
        self.op(eng, lambda e: e.memset(ap, val), [], [ap])

    def finalize(self):
        import heapq
        nc = self.nc
        assert self.open_pe is None, "PE group left open (last PE instruction must have inc=True)"
        U = self.units
        n = len(U)
        succ = [[] for _ in range(n)]
        ndep = [0] * n
        for i, u in enumerate(U):
            u["deps"].discard(i)
            ndep[i] = len(u["deps"])
            for d in u["deps"]:
                succ[d].append(i)
        blevel = [0.0] * n
        for i in range(n - 1, -1, -1):
            u = U[i]
            own = u["cost"] + (u["dma_lat"] if u["dma_sem"] else 0.0)
            m = 0.0
            for j in succ[i]:
                if blevel[j] > m:
                    m = blevel[j]
            blevel[i] = own + self.SEM_LAT + m
        ready_t = [0.0] * n
        fin = [0.0] * n
        free = {e: 0.0 for e in self.engs}
        avail = {e: [] for e in self.engs}
        for i, u in enumerate(U):
            if ndep[i] == 0:
                avail[u["eng"]].append(i)
        order = {e: [] for e in self.engs}
        done = 0
        while done < n:
            best = None
            for e in self.engs:
                if not avail[e]:
                    continue
                t = max(free[e], min(ready_t[i] for i in avail[e]))
                if best is None or t < best[0]:
                    best = (t, e)
            assert best is not None, "scheduler stuck (dependency cycle?)"
            T, e = best
            cand = [i for i in avail[e] if ready_t[i] <= T]
            if PRIO_MODE == 0:
                i = min(cand)
            else:
                i = max(cand, key=lambda c: (blevel[c], -c))
            avail[e].remove(i)
            u = U[i]
            free[e] = T + u["cost"]
            fin[i] = T + (u["cost"] + u["dma_lat"] if u["dma_sem"] else u["cost"])
            order[e].append(i)
            done += 1
            for j in succ[i]:
                ready_t[j] = max(ready_t[j], fin[i] + self.SEM_LAT)
                ndep[j] -= 1
                if ndep[j] == 0:
                    avail[U[j]["eng"]].append(j)
        self.sim_makespan = max(fin) if n else 0.0
        cnt = {}
        semof = [None] * n
        valof = [0] * n
        for e in self.engs:
            for i in order[e]:
                u = U[i]
                sk = u["dma_sem"] if u["dma_sem"] else e
                inc = 16 if u["dma_sem"] else 1
                cnt[sk] = cnt.get(sk, 0) + inc
                semof[i], valof[i] = sk, cnt[sk]
        es = contextlib.ExitStack()
        sems = {k: es.enter_context(nc.semaphore("s_" + k)) for k in cnt}
        streams = {}
        for e in self.engs:
            seen = {}
            st = []
            for i in order[e]:
                u = U[i]
                need = {}
                for d in u["deps"]:
                    sk, v = semof[d], valof[d]
                    if sk == "pe" and e == "pe":
                        continue
                    if v > need.get(sk, 0):
                        need[sk] = v
                for sk, v in need.items():
                    if seen.get(sk, 0) < v:
                        seen[sk] = v
                        st.append(("wait", sk, v))
                nf = len(u["fns"])
                for j, fn in enumerate(u["fns"]):
                    st.append(("op", fn, semof[i] if j == nf - 1 else None, 16 if u["dma_sem"] else 1))
            streams[e] = st
        for i, u in enumerate(U):
            if u["dma_sem"] and u.get("final"):
                pass
        fin_w = {}
        for i, u in enumerate(U):
            if u["dma_sem"] and u.get("final"):
                fin_w[semof[i]] = max(fin_w.get(semof[i], 0), valof[i])
        for sk, v in fin_w.items():
            streams["sp"].append(("wait", sk, v))
        for e in ("pe", "act", "dve", "pool"):
            if cnt.get(e, 0) > 0:
                streams["sp"].append(("wait", e, cnt[e]))
        engobj = {"pe": "tensor", "act": "scalar", "dve": "vector", "pool": "gpsimd", "sp": "sync"}
        with es, nc.Block() as block:
            def run(engname):
                def body(e):
                    for it in streams[engname]:
                        if it[0] == "wait":
                            e.wait_ge(sems[it[1]], it[2])
                        else:
                            ins = it[1](e)
                            if it[2] is not None:
                                ins.then_inc(sems[it[2]], it[3])
                return body
            block.sync(run("sp"))
            block.tensor(run("pe"))
            block.scalar(run("act"))
            block.vector(run("dve"))
            block.gpsimd(run("pool"))


N_THREADS = 2
PRIO_MODE = 1
LAG = 9


def build_program():
    nc = bass.Bass("TRN2", target_bir_lowering=False)
    P = Prog(nc)

    def din(name, shape):
        P.dram.add(name)
        return nc.dram_tensor(name, list(shape), F32, kind="ExternalInput").ap()

    def dout(name, shape):
        P.dram.add(name)
        return nc.dram_tensor(name, list(shape), F32, kind="ExternalOutput").ap()

    xp = din("xp", [CTX_PER_CORE, 256, D])
    xs = din("xs", [4096, D])
    cv_d = din("cv", [2, D])
    ck_d = din("ck", [512, 128])
    cvv_d = din("cvv", [512, 128])
    st_d = din("st", [2, 4, 64, 128])
    wmod_d = din("wmod", [D, 3 * D])
    bmod_d = din("bmod", [3 * D])
    ng_d = din("ng", [D])
    win_d = din("win", [D, WIN_COLS])
    wlrs_d = din("wlrs", [D, 32])
    wgk_d = din("wgk", [2, 33, 512])
    sink_d = din("sink", [8])
    gng_d = din("gng", [128])
    wout_d = din("wout", [D, D])
    fng_d = din("fng", [D])
    ident_d = din("ident", [128, 128])
    tri_d = din("tri", [2, 128, 128])
    chsel_d = din("chsel", [128, 2])
    gmask_d = din("gmask", [2, 128, 128])
    wmask_d = din("wmask", [2, 128, 128])
    rope_d = din("rope", [2, 17 * 128, 64])

    yp_o = dout("yp", [CTX_PER_CORE, 256, D])
    ys_o = dout("ys", [2048, D])
    nk_o = dout("nk", [CTX_PER_CORE, 256, 128])
    nv_o = dout("nv", [CTX_PER_CORE, 256, 128])
    ns_o = dout("ns", [CTX_PER_CORE, 2, 4, 64, 128])

    scr_h = [nc.dram_tensor("scr_h%d" % i, [128, 1024], BF16, kind="Internal").ap() for i in range(24)]
    scr_v = [nc.dram_tensor("scr_v%d" % i, [128, 512], BF16, kind="Internal").ap() for i in range(24)]
    scr_k = [nc.dram_tensor("scr_k%d" % i, [128, 256], F32, kind="Internal").ap() for i in range(24)]

    sb = lambda name, shape, dt=F32: nc.alloc_sbuf_tensor("sb_" + name, list(shape), dt)

    winP = sb("winP", [128, 8, WP_COLS], BF16)
    winM = sb("winM", [128, 8, WIN_COLS - WP_COLS], BF16)

    def wslice(k, c0, c1):
        if c1 <= WP_COLS:
            return winP[:, k, c0:c1]
        assert c0 >= WP_COLS
        return winM[:, k, c0 - WP_COLS:c1 - WP_COLS]
    wlrs = sb("wlrs_sb", [128, 8, 32], BF16)
    wout = sb("wout_sb", [128, 8, D], BF16)
    wgk = [sb("wgk_sb%d" % i, [33, 512], BF16) for i in range(2)]
    ident = sb("ident_sb", [128, 128], BF16)
    tri = [sb("tri_sb%d" % i, [128, 128], F32) for i in range(2)]
    chsel = sb("chsel_sb", [128, 2], F32)
    gmask2 = sb("gmask2", [128, 2, 128], BF16)
    wmask = [sb("wmask_sb%d" % i, [128, 128], BF16) for i in range(2)]
    ropec = sb("ropec", [128, 17, 64], F32)
    ropes = sb("ropes", [128, 17, 64], F32)
    gate_bc = [sb("gate_bc%d" % i, [128, D], F32) for i in range(2)]
    fng_bc = sb("fng_bc", [128, D], F32)
    gng_bc = sb("gng_bc", [128, 128], F32)
    esink = sb("esink", [128, 8], F32)
    gmod = sb("gmod", [128, 8, 2], F32)
    shiftv = sb("shiftv", [128, 8, 2], F32)
    ckT = sb("ckT", [128, 512], BF16)
    cva = sb("cva", [128, 4, 2, 65], BF16)
    kT_st = [sb("kT_st%d" % i, [128, 128], BF16) for i in range(17)]
    va_st = [sb("va_st%d" % i, [128, 2, 65], BF16) for i in range(17)]
    sprev = [sb("sprev%d" % i, [128, 2, 128], BF16) for i in range(32)]
    cvT = sb("cvT", [128, 8, 2], F32)
    cve = sb("cve", [128, 8, 2], F32)
    scT = sb("scT", [128, 8, 2], F32)
    bmT = sb("bmT", [128, 24], F32)
    ngT = sb("ngT", [128, 8], F32)
    modT = sb("modT", [128, 16, 2], F32)
    sink_bc = sb("sink_bc", [128, 8], F32)

    class TR:
        pass

    B = [nc.alloc_psum_tensor("bank%d" % i, [128, 512], F32) for i in range(8)]
    threads = []
    for th in range(N_THREADS):
        R = TR()
        R.th = th
        n = lambda s_: "%s_t%d" % (s_, th)
        R.b = [B[4 * th + i] for i in range(4)] if N_THREADS == 2 else [B[i] for i in range(4)]
        R.bb = [x[:].bitcast(BF16) for x in R.b]
        R.xt = [sb(n("xt%d" % i), [128, D], F32) for i in range(2)]
        R.slot = 0
        R.xn = sb(n("xn"), [128, D], BF16)
        R.hTa = sb(n("hTa"), [128, 4, 128], BF16)
        R.hTb = sb(n("hTb"), [128, 4, 128], BF16)
        R.F = [sb(n("F%d" % i), [128, 512], F32) for i in range(6)]
        R.q_sb = R.e_g = R.oa = R.F[0]
        R.qA = R.spl = R.on = R.F[1]
        R.qB = R.Eg = R.F[2]
        R.e_z = R.Einv = R.F[3]
        R.sz = [R.F[4], R.F[5]]
        R.xg = sb(n("xg"), [128, D], F32)
        R.xr = R.xg
        R.qr = sb(n("qr"), [128, 512], BF16)
        R.qT = sb(n("qT"), [128, 8, 128], BF16)
        R.kr = R.qr[:, 0:128]
        R.vg_bf = sb(n("vg_bf"), [128, 512], BF16)
        R.lrT = sb(n("lrT"), [33, 128], BF16)
        R.qd = sb(n("qd"), [128, 512], BF16)
        R.ki = sb(n("ki"), [128, 512], BF16)
        R.qkT = sb(n("qkT"), [128, 8, 128], BF16)
        R.yT = R.qkT
        R.Am2 = sb(n("Am2"), [128, 2, 4, 128], BF16)
        R.PT = [sb(n("PT%d" % i), [128, 512], BF16) for i in range(3)]
        R.pti = 0
        R.ya = sb(n("ya"), [128, 512], BF16)
        R.kvo = R.F[2][:, 0:256]
        R.stat = sb(n("stat"), [128, 32], F32)
        R.ss, R.lnv, R.rstd = R.stat[:, 0:1], R.stat[:, 1:2], R.stat[:, 2:3]
        R.ss2, R.lnv2, R.rstd2 = R.stat[:, 3:4], R.stat[:, 4:5], R.stat[:, 5:6]
        R.sso, R.lno, R.rso = R.stat[:, 8:12], R.stat[:, 12:16], R.stat[:, 16:20]
        R.den, R.rden = R.stat[:, 20:28], sb(n("rden"), [128, 8], F32)
        R.egt = sb(n("egt"), [128, 8], F32)
        R.kg_sb = sb(n("kg_sb"), [128, 256], F32)
        R.yg = R.kg_sb[:].bitcast(BF16)
        R.S_f = sb(n("S_f"), [128, 2, 128], F32)
        R.S_b = sb(n("S_b"), [128, 2, 128], F32)
        R.Sbf = [sb(n("Sbf%d" % i), [128, 2, 128], BF16) for i in range(2)]
        threads.append(R)
    R0, R1 = threads[0], threads[-1]

    P.dma("pool", ident[:], ident_d)
    for i in range(2):
        P.dma("pool", wgk[i][:], wgk_d[i])
        P.dma("pool", gmask2[:, i, :], gmask_d[i])
        P.dma("pool", wmask[i][:], wmask_d[i])
        P.dma("sp", tri[i][:], tri_d[i])
    P.dma("sp", chsel[:], chsel_d)
    P.dma("sp", ropec[:], rope_d[0].rearrange("(t p) d -> p t d", p=128))
    P.dma("sp", ropes[:], rope_d[1].rearrange("(t p) d -> p t d", p=128))
    P.dma("sp", fng_bc[:], fng_d.partition_broadcast(128))
    P.dma("sp", gng_bc[:], gng_d.partition_broadcast(128))
    P.dma("sp", sink_bc[:], sink_d.partition_broadcast(128))
    for w in range(2):
        P.dma("sp", cvT[:, :, w], cv_d[w].rearrange("(c p) -> p c", p=128), slow=True)
    P.dma("sp", bmT[:], bmod_d.rearrange("(c p) -> p c", p=128), slow=True)
    P.dma("sp", ngT[:], ng_d.rearrange("(c p) -> p c", p=128), slow=True)
    for t in range(4):
        P.dma("pool", cva[:, t, :, 0:64], cvv_d[t * 128:(t + 1) * 128, :].rearrange("p (k d) -> p k d", k=2))

    P.memset("pool", cva[:, :, :, 64:65], 1.0)
    for i in range(17):
        P.memset("pool", va_st[i][:, :, 64:65], 1.0)
    for R in threads:
        P.memset("pool", R.lrT[:], 1.0)
        P.memset("pool", R.qT[:], 0.0)
    P.act(esink[:], sink_bc[:], AF.Exp)

    P.act(cve[:], cvT[:], AF.Exp, scale=-1.0)
    P.ts("dve", cve[:], cve[:], 1.0, ALU.add)
    P.recip(cve[:], cve[:])
    P.tt("dve", scT[:], cvT[:], cve[:], ALU.mult)
    wst1 = [[f[:] for f in R0.F[0:4]], [f[:] for f in R1.F[0:4]], [R0.F[4][:], R0.F[5][:], R1.F[4][:], R1.F[5][:]]]
    MB = R0.b
    for k in range(8):
        bufs = wst1[k % 3]
        for p4 in range(4):
            P.dma("sp", bufs[p4], wmod_d[k * 128:(k + 1) * 128, p4 * 512:(p4 + 1) * 512])
        for n_ in range(16):
            P.mm(MB[1][:, n_ * 2:(n_ + 1) * 2], lhsT=bufs[n_ // 4][:, (n_ % 4) * 128:(n_ % 4 + 1) * 128], rhs=scT[:, k, :],
                 start=(k == 0 and n_ == 0), stop=(k == 7), inc=(n_ == 15), sgc=True)
    f32v = lambda t: t[:].bitcast(F32)
    bgrow = [R0.Am2[:].rearrange("p a h t -> p (a h t)").bitcast(F32)[0:2, :], R1.Am2[:].rearrange("p a h t -> p (a h t)").bitcast(F32)[0:2, :]]
    grow = [R0.qkT[:].rearrange("p a t -> p (a t)").bitcast(F32)[0:2, :], R1.qkT[:].rearrange("p a t -> p (a t)").bitcast(F32)[0:2, :]]
    for j in range(2):
        P.dma("sp", bgrow[j], bmod_d[2 * D + j * 512:2 * D + (j + 1) * 512].partition_broadcast(2))
    sel = R1.ya[:].bitcast(F32)[0:2, 0:256].rearrange("p (w t) -> p w t", w=2)
    P.memset("dve", sel, 0.0)
    P.memset("dve", sel[0:1, 0, :], 1.0)
    P.ts("dve", sel[:, 1, :], sel[:, 0, :], -1.0, ALU.mult, 1.0, ALU.add)
    for k in range(8):
        P.dma("pool", winP[:, k, :], win_d[k * 128:(k + 1) * 128, 0:WP_COLS])
    for k in range(8):
        P.dma("pool", wlrs[:, k, :], wlrs_d[k * 128:(k + 1) * 128, :])
    for k in range(8):
        P.dma("pool", winM[:, k, :], win_d[k * 128:(k + 1) * 128, WP_COLS:WIN_COLS])
    for k in range(8):
        P.dma("pool", wout[:, k, :], wout_d[k * 128:(k + 1) * 128, :])
    P.tt("dve", modT[:], MB[1][:, 0:32].rearrange("p (n w) -> p n w", w=2),
         bass.AP(bmT, 0, [[24, 128], [1, 16], [0, 2]]), ALU.add)
    P.cp("dve", shiftv[:], modT[:, 0:8, :])
    P.ts("dve", modT[:, 8:16, :], modT[:, 8:16, :], 1.0, ALU.add)
    P.tt("dve", gmod[:], modT[:, 8:16, :], bass.AP(ngT, 0, [[8, 128], [1, 8], [0, 2]]), ALU.mult)
    GB = [R1.b[2], R1.b[3]]
    wstG = [R0.xg, R1.xg]
    for k in range(8):
        buf = wstG[k % 2]
        P.dma("sp", buf[:], wmod_d[k * 128:(k + 1) * 128, 2 * D:3 * D])
        for j in range(2):
            P.mm(GB[j][0:2, :], lhsT=scT[:, k, :], rhs=buf[:, j * 512:(j + 1) * 512],
                 start=(k == 0), stop=(k == 7), inc=(j == 1))
    for j in range(2):
        P.tt("dve", grow[j], GB[j][0:2, :], bgrow[j], ALU.add)
    for w in range(2):
        for j in range(2):
            P.mm(GB[j][:, :], lhsT=sel[:, w, :], rhs=grow[j], start=True, stop=True, inc=(j == 1))
        for j in range(2):
            P.cp("act", gate_bc[w][:, j * 512:(j + 1) * 512], GB[j][:, :])
    cktmp = R0.F[4]
    P.dma("sp", cktmp[:, 0:512].rearrange("p (t c) -> p t c", t=4), ck_d.rearrange("(t p) c -> p t c", p=128))
    ck_bf = R0.qr[:].rearrange("p (t c) -> p t c", t=4)
    P.cp("dve", ck_bf, cktmp[:, 0:512].rearrange("p (t c) -> p t c", t=4))
    for t in range(4):
        P.tr(R0.bb[0][:, t * 128:(t + 1) * 128], ck_bf[:, t, :], ident[:], inc=(t == 3))
    P.cp("dve", ckT[:], R0.bb[0][:, 0:512])

    def hT(R, k):
        return (R.hTa if k < 4 else R.hTb)[:, k % 4, :]

    def prep_h(R, which):
        x = R.xt[R.slot]
        P.act(R.xn[:], x[:], AF.Square, accum_out=R.ss)
        P.act(R.lnv, R.ss, AF.Ln, scale=1.0 / D, bias=EPS)
        P.act(R.rstd, R.lnv, AF.Exp, scale=-0.5)
        P.ts("dve", R.xn[:], x[:], R.rstd, ALU.mult)
        for k in range(8):
            P.tr(R.bb[0][:, k * 128:(k + 1) * 128], R.xn[:, k * 128:(k + 1) * 128], ident[:], inc=(k == 7))
        yield
        for k in range(8):
            dst = hT(R, k)
            src = R.bb[0][:, k * 128:(k + 1) * 128]
            if k < 6:
                P.act(dst, src, AF.Identity, scale=gmod[:, k, which:which + 1], bias=shiftv[:, k, which:which + 1])
            else:
                P.ts("dve", dst, src, gmod[:, k, which:which + 1], ALU.mult, shiftv[:, k, which:which + 1], ALU.add)

    def proj(R, bank_ap, c0, ncols):
        for k in range(8):
            P.mm(bank_ap, lhsT=hT(R, k), rhs=wslice(k, c0, c0 + ncols), start=(k == 0), stop=(k == 7), inc=(k == 7))

    def gates(R, which, dirs):
        G = R.b[3]
        for k in range(8):
            lhs = wslice(k, C_LR, C_LR + 32) if which == 0 else wlrs[:, k, :]
            P.mm(G[0:32, 0:128], lhsT=lhs, rhs=hT(R, k), start=(k == 0), stop=(k == 7), inc=(k == 7))
        P.cp("dve", R.lrT[0:32, :], G[0:32, 0:128])
        c0, c1 = (0, 512) if dirs == "fb" else (256, 512)
        P.mm(G[:, c0:c1], lhsT=R.lrT[:, :], rhs=wgk[which][:, c0:c1], start=True, stop=True, inc=True)
        P.act(R.e_g[:, c0:c1], G[:, c0:c1], AF.Exp, scale=-1.0)
        P.act(R.spl[:, c0:c1], R.e_g[:, c0:c1], AF.Ln, bias=1.0)
        if dirs == "fb":
            P.mm(G[:, 0:256], lhsT=tri[0][:], rhs=R.spl[:, 0:256], start=True, stop=True, inc=False)
        P.mm(G[:, 256:512], lhsT=tri[1][:], rhs=R.spl[:, 256:512], start=True, stop=True, inc=True)

    def gtot(R, dirs, bank_ap):
        dl = (0, 1) if dirs == "fb" else ((0,) if dirs == "f" else (1,))
        for d_ in dl:
            for hp in range(2):
                col = (d_ * 2 + hp) * 2
                last = (d_ == dl[-1] and hp == 1)
                P.mm(bank_ap[:, col:col + 2], lhsT=R.spl[:, d_ * 256 + hp * 128:d_ * 256 + (hp + 1) * 128], rhs=chsel[:],
                     start=True, stop=True, inc=last)
        c0, c1 = {"fb": (0, 8), "f": (0, 4), "b": (4, 8)}[dirs]
        P.act(R.egt[:, c0:c1], bank_ap[:, c0:c1], AF.Exp)

    def rope(src_f32, A_, B_, t_idx, nheads, out_ap, a_view=None, b_view=None):
        s3 = src_f32.rearrange("p (h d) -> p h d", d=64)
        cosb = bass.AP(ropec, t_idx * 64, [[17 * 64, 128], [0, nheads], [1, 64]])
        P.tt("pool", A_.rearrange("p (h d) -> p h d", d=64), s3, cosb, ALU.mult)
        base = src_f32.offset
        pitch = src_f32.ap[0][0]
        for half in range(2):
            o_ = bass.AP(B_.tensor, B_.offset + half * 16, [[B_.ap[0][0], 128], [64, nheads], [32, 2], [1, 16]])
            i_ = bass.AP(src_f32.tensor, base + (1 - half) * 16, [[pitch, 128], [64, nheads], [32, 2], [1, 16]])
            s_ = bass.AP(ropes, t_idx * 64 + half * 16, [[17 * 64, 128], [0, nheads], [32, 2], [1, 16]])
            P.tt("pool", o_, i_, s_, ALU.mult)
        P.tt("pool", out_ap, A_ if a_view is None else a_view, B_ if b_view is None else b_view, ALU.add)

    def silu_dve(R, bank, out_buf):
        e = R.e_z
        P.act(e[:], bank[:], AF.Exp, scale=-1.0)
        P.ts("dve", e[:], e[:], 1.0, ALU.add)
        P.recip(e[:], e[:])
        P.tt("dve", out_buf[:], bank[:], e[:], ALU.mult)

    def silu_act(R, bank, out_buf):
        e = R.e_z
        P.act(e[:], bank[:], AF.Exp, scale=-1.0)
        P.act(e[:], e[:], AF.Ln, bias=1.0)
        P.act(e[:], e[:], AF.Exp, scale=-1.0)
        P.tt("dve", out_buf[:], bank[:], e[:], ALU.mult)

    def prepass_tile(R, which, is_smp, t_idx, own, need_kv, sprev_base, kv_slot, S_b, kvout=None, scr_id=None):
        b = R.b
        yield from prep_h(R, which)
        if scr_id is not None:
            sk = "scr%d" % scr_id
            P.dma("sp", scr_h[scr_id][:, 0:512], R.hTa[:].rearrange("p a t -> p (a t)"), semkey="st_hTa_t%d" % R.th)
            P.dma("sp", scr_h[scr_id][:, 512:1024], R.hTb[:].rearrange("p a t -> p (a t)"), semkey="st_hTb_t%d" % R.th)
        proj(R, b[1][:], C_VG, 512)
        yield
        P.cp("act", R.vg_bf[:], b[1][:])
        if scr_id is not None:
            P.dma("sp", scr_v[scr_id], R.vg_bf[:], semkey="st_vg_t%d" % R.th)
        if need_kv:
            proj(R, b[2][:], C_KG, 512)
        else:
            proj(R, b[2][:, 0:256], C_KG, 256)
        if scr_id is not None:
            P.cp("act", R.kg_sb[:], b[2][:, 0:256])
            P.dma("sp", scr_k[scr_id], R.kg_sb[:], semkey="st_kg_t%d" % R.th)
        yield
        gates(R, which, "b")
        yield
        gtot(R, "b", b[1])
        P.act(R.Einv[:, 256:512], b[3][:, 256:512], AF.Exp, scale=-1.0)
        P.tt("dve", R.ki[:, 256:512], b[2][:, 0:256], R.Einv[:, 256:512], ALU.mult)
        yield
        if need_kv:
            if kvout is not None:
                P.cp("act", R.kvo, b[2][:, 256:512])
                P.dma("sp", kvout[0], R.kvo[:, 0:128])
                P.dma("sp", kvout[1], R.kvo[:, 128:256])
            if is_smp:
                ks = R.q_sb[:, 0:128]
                P.cp("act", ks, b[2][:, 256:384])
                rope(ks, R.qA[:, 0:128], R.qB[:, 0:128], t_idx, 2, R.kr)
            else:
                P.cp("act", R.kr, b[2][:, 256:384])
            P.cp("dve", va_st[kv_slot][:, :, 0:64], b[2][:, 384:512].rearrange("p (k d) -> p k d", k=2))
            P.tr(R.bb[0][:, 0:128], R.kr, ident[:], inc=True)
            P.cp("dve", kT_st[kv_slot][:], R.bb[0][:, 0:128])
        yield
        DBs = [b[3], b[1]]
        for c in (1, 0):
            DB = DBs[c]
            if own:
                P.cp("pool", sprev[sprev_base + c][:], S_b[:])
            for h in range(4):
                hp, hh = h // 2, h % 2
                P.mm(DB[hh * 64:(hh + 1) * 64, hp * 128:(hp + 1) * 128],
                     lhsT=R.ki[c * 64:(c + 1) * 64, 256 + h * 64:256 + (h + 1) * 64],
                     rhs=R.vg_bf[c * 64:(c + 1) * 64, h * 128:(h + 1) * 128], start=True, stop=True, inc=(h == 3))
            for hp in range(2):
                col = (2 + hp) * 2 + c
                P.tt("dve", S_b[:, hp, :], S_b[:, hp, :], DB[:, hp * 128:(hp + 1) * 128], ALU.add)
                P.ts("dve", S_b[:, hp, :], S_b[:, hp, :], R.egt[:, col:col + 1], ALU.mult)
            if c == 1:
                yield

    def main_tile(R, which, is_smp, t_idx, sprev_base, key_list, y_dst, S_f, scr_id):
        b, bb = R.b, R.bb
        x = R.xt[R.slot]
        P.dma("sp", R.hTa[:].rearrange("p a t -> p (a t)"), scr_h[scr_id][:, 0:512])
        P.dma("sp", R.hTb[:].rearrange("p a t -> p (a t)"), scr_h[scr_id][:, 512:1024])
        P.dma("sp", R.vg_bf[:], scr_v[scr_id])
        P.dma("sp", R.kg_sb[:], scr_k[scr_id])
        yield
        proj(R, b[1][:], C_Q, 512)
        yield
        qperm = bass.AP(R.qr, 0, [[512, 128], [64, 2], [128, 4], [1, 64]])
        if is_smp:
            P.cp("act", R.q_sb[:], b[1][:])
            rope(R.q_sb[:], R.qA[:], R.qB[:], t_idx, 8, qperm,
                 a_view=R.qA[:].rearrange("p (k g d) -> p k g d", k=2, g=4),
                 b_view=R.qB[:].rearrange("p (k g d) -> p k g d", k=2, g=4))
        else:
            P.cp("act", qperm, b[1][:].rearrange("p (k g d) -> p k g d", k=2, g=4))
        proj(R, b[2][:], C_ZA, 512)
        yield
        silu_act(R, b[2], R.sz[0])
        proj(R, b[1][:], C_ZG, 512)
        yield
        silu_act(R, b[1], R.sz[1])
        P.tt("pool", R.sz[1][:].rearrange("p (h v) -> p h v", h=4), R.sz[1][:].rearrange("p (h v) -> p h v", h=4),
             bass.AP(gng_bc, 0, [[128, 128], [0, 4], [1, 128]]), ALU.mult)
        yield
        proj(R, b[1][:, 0:256], C_QG, 256)
        for g in range(4):
            P.tr(bb[0][:, g * 128:(g + 1) * 128], R.qr[:, g * 128:(g + 1) * 128], ident[:], inc=(g == 3))
        yield
        P.cp("dve", R.qT[0:64, 0:4, :], bb[0][0:64, 0:512].rearrange("p (g t) -> p g t", g=4))
        P.cp("dve", R.qT[64:128, 4:8, :], bb[0][64:128, 0:512].rearrange("p (g t) -> p g t", g=4))
        gates(R, which, "fb")
        yield
        gtot(R, "f", b[2])
        P.act(R.Eg[:], b[3][:], AF.Exp)
        P.act(R.Einv[:], b[3][:], AF.Exp, scale=-1.0)
        qg_b = bass.AP(b[1], 0, [[512, 128], [0, 2], [1, 256]])
        kg_b = bass.AP(R.kg_sb, 0, [[256, 128], [0, 2], [1, 256]])
        P.stt("dve", R.qd[:].rearrange("p (a c) -> p a c", a=2), qg_b, 0.125, R.Eg[:].rearrange("p (a c) -> p a c", a=2), ALU.mult, ALU.mult)
        P.tt("dve", R.ki[:].rearrange("p (a c) -> p a c", a=2), kg_b, R.Einv[:].rearrange("p (a c) -> p a c", a=2), ALU.mult)
        yield
        for i in range(4):
            P.tr(bb[0][:, i * 128:(i + 1) * 128], R.qd[:, i * 128:(i + 1) * 128], ident[:], inc=False)
        for i in range(4):
            P.tr(bb[0][:, (4 + i) * 128:(5 + i) * 128], R.ki[:, i * 128:(i + 1) * 128], ident[:], inc=(i == 3))
        yield
        P.cp("act", R.qkT[:, 0:4, :], bb[0][:, 0:512].rearrange("p (i t) -> p i t", i=4))
        P.cp("act", R.qkT[:, 4:8, :], bb[0][:, 512:1024].rearrange("p (i t) -> p i t", i=4))
        AB = [b[1], b[2]]
        for hh in range(2):
            for d_ in range(2):
                for hp in range(2):
                    last = (d_ == 1 and hp == 1)
                    P.mm(AB[hh][:, (d_ * 2 + hp) * 128:(d_ * 2 + hp + 1) * 128], lhsT=R.qkT[hh * 64:(hh + 1) * 64, 4 + d_ * 2 + hp, :],
                         rhs=R.qkT[hh * 64:(hh + 1) * 64, d_ * 2 + hp, :], start=True, stop=True, inc=last)
        yield
        for hh in range(2):
            P.tt("dve", bass.AP(R.Am2, hh * 128, [[1024, 128], [512, 2], [256, 2], [1, 128]]),
                 AB[hh][:].rearrange("p (a c t) -> p a c t", a=2, c=2),
                 bass.AP(gmask2, 0, [[256, 128], [128, 2], [0, 2], [1, 128]]), ALU.mult)
        OBs = [b[1], b[2]]
        P.cp("pool", R.Sbf[0][:], S_f[:])
        for h in range(4):
            hp, hh = h // 2, h % 2
            OB = OBs[hh]
            oh = OB[:, hp * 128:(hp + 1) * 128]
            P.mm(oh, lhsT=R.Am2[:, 0, h, :], rhs=R.vg_bf[:, h * 128:(h + 1) * 128], start=(hp == 0), stop=False, sgc=True)
            P.mm(oh, lhsT=R.Am2[:, 1, h, :], rhs=R.vg_bf[:, h * 128:(h + 1) * 128], start=False, stop=False, sgc=True)
            for c in range(2):
                P.mm(OB[c * 64:(c + 1) * 64, hp * 128:(hp + 1) * 128],
                     lhsT=R.qkT[hh * 64:(hh + 1) * 64, 2 + hp, c * 64:(c + 1) * 64],
                     rhs=sprev[sprev_base + c][hh * 64:(hh + 1) * 64, hp, :], start=False, stop=False, sgc=True)
            P.mm(OB[0:64, hp * 128:(hp + 1) * 128], lhsT=R.qkT[hh * 64:(hh + 1) * 64, hp, 0:64],
                 rhs=R.Sbf[0][hh * 64:(hh + 1) * 64, hp, :], start=False, stop=False, inc=(h == 3), sgc=True)
        yield
        DSB = [b[3], b[0]]
        for c in range(2):
            for h in range(4):
                hp, hh = h // 2, h % 2
                P.mm(DSB[c][hh * 64:(hh + 1) * 64, hp * 128:(hp + 1) * 128],
                     lhsT=R.ki[c * 64:(c + 1) * 64, h * 64:(h + 1) * 64],
                     rhs=R.vg_bf[c * 64:(c + 1) * 64, h * 128:(h + 1) * 128], start=True, stop=True, inc=(h == 3))
            for hp in range(2):
                col = hp * 2 + c
                P.tt("dve", S_f[:, hp, :], S_f[:, hp, :], DSB[c][:, hp * 128:(hp + 1) * 128], ALU.add)
                P.ts("dve", S_f[:, hp, :], S_f[:, hp, :], R.egt[:, col:col + 1], ALU.mult)
            if c == 0:
                P.cp("pool", R.Sbf[1][:], S_f[:])
                for h in range(4):
                    hp, hh = h // 2, h % 2
                    P.mm(OBs[hh][64:128, hp * 128:(hp + 1) * 128], lhsT=R.qkT[hh * 64:(hh + 1) * 64, hp, 64:128],
                         rhs=R.Sbf[1][hh * 64:(hh + 1) * 64, hp, :], start=False, stop=True, inc=(h == 3), sgc=True)
        yield
        for h in range(4):
            hp, hh = h // 2, h % 2
            P.act(R.xn[:, h * 128:(h + 1) * 128], OBs[hh][:, hp * 128:(hp + 1) * 128], AF.Square, accum_out=R.sso[:, h:h + 1])
        P.act(R.lno, R.sso, AF.Ln, scale=1.0 / 128.0, bias=EPS)
        P.act(R.rso, R.lno, AF.Exp, scale=-0.5)
        for hh in range(2):
            P.tt("dve", bass.AP(R.on, hh * 128, [[512, 128], [256, 2], [1, 128]]),
                 OBs[hh][:, 0:256].rearrange("p (a v) -> p a v", a=2),
                 bass.AP(R.stat, 16 + hh, [[32, 128], [2, 2], [0, 128]]), ALU.mult)
        P.tt("pool", R.yg[:], R.on[:], R.sz[1][:], ALU.mult)
        yield
        nk = len(key_list)
        SB = [b[0], b[3]]
        items = [(kv, i) for kv in range(2) for i in range(nk)]

        def issue_scores(n_):
            kv, i = items[n_]
            kT_ap = key_list[i][0]
            P.mm(SB[n_ % 2][:], lhsT=kT_ap[:, :], rhs=R.qT[:, kv * 4:(kv + 1) * 4, :],
                 start=True, stop=True, inc=True)

        issue_scores(0)
        for n_, (kv, i) in enumerate(items):
            OA = b[1 + kv]
            kT_ap, va_ap, mi = key_list[i]
            if n_ + 1 < len(items):
                issue_scores(n_ + 1)
            pt = R.PT[R.pti]
            R.pti = (R.pti + 1) % 3
            P.act(pt[:], SB[n_ % 2][:], AF.Exp, scale=0.125)
            if mi is not None:
                P.tt("pool", pt[:].rearrange("p (g t) -> p g t", g=4), pt[:].rearrange("p (g t) -> p g t", g=4),
                     bass.AP(wmask[mi], 0, [[128, 128], [0, 4], [1, 128]]), ALU.mult)
            for g in range(4):
                P.mm(OA[:, g * 65:(g + 1) * 65], lhsT=pt[:, g * 128:(g + 1) * 128], rhs=va_ap[:, kv, :],
                     start=(i == 0 and g == 0), stop=(i == nk - 1), inc=(g == 3), sgc=True)
            if (nk > 3 and i == 3) or (nk <= 3 and i == nk - 1):
                yield
            if i == nk - 1:
                o3 = OA[:, 0:260].rearrange("p (g e) -> p g e", g=4)
                P.tt("dve", R.den[:, kv * 4:(kv + 1) * 4], o3[:, :, 64], esink[:, kv * 4:(kv + 1) * 4], ALU.add)
                P.recip(R.rden[:, kv * 4:(kv + 1) * 4], R.den[:, kv * 4:(kv + 1) * 4])
                P.tt("dve", R.oa[:, kv * 256:(kv + 1) * 256].rearrange("p (g d) -> p g d", g=4), o3[:, :, 0:64],
                     bass.AP(R.rden, kv * 4, [[8, 128], [1, 4], [0, 64]]), ALU.mult)
                yield
        P.tt("pool", R.ya[:], R.oa[:], R.sz[0][:], ALU.mult)
        for i in range(4):
            P.tr(bb[0][:, i * 128:(i + 1) * 128], R.ya[:, i * 128:(i + 1) * 128], ident[:], inc=False)
        for i in range(4):
            P.tr(bb[0][:, (4 + i) * 128:(5 + i) * 128], R.yg[:, i * 128:(i + 1) * 128], ident[:], inc=(i == 3))
        yield
        P.cp("act", R.yT[:, 0:4, :], bb[0][:, 0:512].rearrange("p (i t) -> p i t", i=4))
        P.cp("dve", R.yT[:, 4:8, :], bb[0][:, 512:1024].rearrange("p (i t) -> p i t", i=4))
        for j in range(2):
            for k in range(8):
                P.mm(b[1 + j][:], lhsT=R.yT[:, k, :], rhs=wout[:, k, j * 512:(j + 1) * 512], start=(k == 0), stop=(k == 7), inc=(k == 7))
        yield
        for j in range(2):
            P.tt("dve", R.xg[:, j * 512:(j + 1) * 512], b[1 + j][:], gate_bc[which][:, j * 512:(j + 1) * 512], ALU.mult)
        P.tt("pool", R.xr[:], R.xg[:], x[:], ALU.add)
        P.act(R.xn[:], R.xr[:], AF.Square, accum_out=R.ss2)
        P.act(R.lnv2, R.ss2, AF.Ln, scale=1.0 / D, bias=EPS)
        P.act(R.rstd2, R.lnv2, AF.Exp, scale=-0.5)
        P.stt("dve", R.xg[:], R.xr[:], R.rstd2, fng_bc[:], ALU.mult, ALU.mult)
        P.dma("sp", y_dst, R.xg[:])

    def st_view(ap3):
        return ap3.rearrange("(hp hh) d v -> (hh d) hp v", hh=2)

    def thread_gen(R, jobs):
        if not jobs:
            return
        R.slot = 0
        P.dma("sp", R.xt[0][:], jobs[0][0])
        for ji, (x_ap, pre, fac, post) in enumerate(jobs):
            if pre is not None:
                pre()
            if ji + 1 < len(jobs):
                P.dma("sp", R.xt[1 - R.slot][:], jobs[ji + 1][0])
            yield from fac()
            if post is not None:
                post()
            R.slot = 1 - R.slot
            yield

    def run_threads(gens, lag):
        active = []
        pending = list(gens)
        step = 0
        while active or pending:
            if pending and (not active or step >= lag * (len(gens) - len(pending))):
                active.append(pending.pop(0))
            for g in list(active):
                try:
                    next(g)
                except StopIteration:
                    active.remove(g)
            step += 1

    def ctx_jobs(R, seqs):
        pre_jobs, main_jobs = [], []
        th = R.th
        for j, s_ in enumerate(seqs):
            kslots = (4 * th + 2 * j, 4 * th + 2 * j + 1)
            spb = 8 * th + 4 * j
            keys_ctx = [(kT_st[kslots[0]], va_st[kslots[0]], None), (kT_st[kslots[1]], va_st[kslots[1]], None)]

            def pre_begin(R=R):
                P.memset("pool", R.S_b[:], 0.0)

            def main_begin(R=R):
                P.memset("pool", R.S_f[:], 0.0)
            pre_jobs.append((xp[s_, 128:256, :], pre_begin,
                             lambda R=R, s_=s_, spb=spb, kslots=kslots: prepass_tile(R, 0, False, 0, True, True, spb + 2, kslots[1], R.S_b,
                                                                       kvout=(nk_o[s_, 128:256, :], nv_o[s_, 128:256, :]), scr_id=2 * s_ + 1), None))
            pre_jobs.append((xp[s_, 0:128, :], None,
                             lambda R=R, s_=s_, spb=spb, kslots=kslots: prepass_tile(R, 0, False, 0, True, True, spb, kslots[0], R.S_b,
                                                                       kvout=(nk_o[s_, 0:128, :], nv_o[s_, 0:128, :]), scr_id=2 * s_),
                             lambda R=R, s_=s_: P.dma("sp", st_view(ns_o[s_, 1]), R.S_b[:])))
            main_jobs.append((xp[s_, 0:128, :], main_begin,
                              lambda R=R, s_=s_, spb=spb, keys_ctx=keys_ctx: main_tile(R, 0, False, 0, spb, keys_ctx, yp_o[s_, 0:128, :], R.S_f, 2 * s_), None))
            main_jobs.append((xp[s_, 128:256, :], None,
                              lambda R=R, s_=s_, spb=spb, keys_ctx=keys_ctx: main_tile(R, 0, False, 0, spb + 2, keys_ctx, yp_o[s_, 128:256, :], R.S_f, 2 * s_ + 1),
                              lambda R=R, s_=s_: P.dma("sp", st_view(ns_o[s_, 0]), R.S_f[:])))
        return pre_jobs + main_jobs

    nT = len(threads)
    seq_split = [list(range(CTX_PER_CORE))[i::nT] for i in range(nT)]
    if PHASES >= 1:
        run_threads([thread_gen(threads[i], ctx_jobs(threads[i], seq_split[i])) for i in range(nT)], LAG)

    S_f, S_b = R0.S_f, R0.S_b
    if PHASES >= 2:
        P.dma("sp", S_f[:], st_view(st_d[0]))
        P.dma("sp", S_b[:], st_view(st_d[1]))
        pj = [[] for _ in range(nT)]
        for n_, t in enumerate(range(31, -1, -1)):
            R = threads[n_ % nT]
            own = t < 16
            pj[n_ % nT].append((xs[t * 128:(t + 1) * 128, :], None,
                                lambda R=R, t=t, own=own: prepass_tile(R, 1, True, min(t, 16), own, t <= 16, (2 * t if own else 0),
                                                                       min(t, 16), S_b, scr_id=(8 + t if own else None)), None))
        run_threads([thread_gen(threads[i], pj[i]) for i in range(nT)], 3)

    if PHASES >= 3:
        mj = [[] for _ in range(nT)]
        for t in range(16):
            R = threads[t % nT]
            keys = []
            if t > 0:
                keys.append((kT_st[t - 1], va_st[t - 1], 0))
            keys.append((kT_st[t], va_st[t], None))
            keys.append((kT_st[t + 1], va_st[t + 1], 1))
            for c4 in range(4):
                keys.append((ckT[:, c4 * 128:(c4 + 1) * 128], cva[:, c4, :, :], None))
            mj[t % nT].append((xs[t * 128:(t + 1) * 128, :], None,
                               lambda R=R, t=t, keys=keys: main_tile(R, 1, True, t, 2 * t, keys, ys_o[t * 128:(t + 1) * 128, :], S_f, 8 + t), None))
        run_threads([thread_gen(threads[i], mj[i]) for i in range(nT)], LAG)

    P.finalize()
    return nc


PHASES = 3

_CACHE = {}


def _consts():
    j = np.arange(128)[:, None]
    i = np.arange(128)[None, :]
    same = (j // 64) == (i // 64)
    tri = np.stack([(same & (j <= i)), (same & (j >= i))]).astype(np.float32) * np.float32(-1.0 / 16.0)
    chsel = np.zeros((128, 2), np.float32)
    chsel[:64, 0] = -1.0 / 16.0
    chsel[64:, 1] = -1.0 / 16.0
    gmask = np.stack([(same & (j <= i)), (same & (j >= i))]).astype(np.float32)
    wmask = np.stack([(j >= i), (j <= i)]).astype(np.float32)
    ident = np.eye(128, dtype=np.float32)
    return tri, chsel, gmask, wmask, ident


def _rope_tables(pos):
    row = (pos // 64).astype(np.float64)
    col = (pos % 64).astype(np.float64)
    nf = 16
    freqs = (10000.0 ** (-np.arange(nf, dtype=np.float32) / np.float32(nf))).astype(np.float32).astype(np.float64)
    ar = (row[:, None].astype(np.float32) * freqs[None, :].astype(np.float32)).astype(np.float64)
    ac = (col[:, None].astype(np.float32) * freqs[None, :].astype(np.float32)).astype(np.float64)
    cos = np.concatenate([np.cos(ar), np.cos(ar), np.cos(ac), np.cos(ac)], 1)
    sin = np.concatenate([-np.sin(ar), np.sin(ar), -np.sin(ac), np.sin(ac)], 1)
    return np.stack([cos, sin]).astype(np.float32)


def kernel(x_prompt, x_sample, c, cache_k, cache_v, state_gla, c_ctx, w_mod, b_mod, norm_g, w_in,
           w_gk_f, b_gk_f, w_gk_b, b_gk_b, sink, gla_norm_g, w_out, final_norm_g):
    f = lambda a: np.ascontiguousarray(np.asarray(a, dtype=np.float32))
    x_prompt, x_sample, c, cache_k, cache_v, state_gla = map(f, (x_prompt, x_sample, c, cache_k, cache_v, state_gla))
    c_ctx, w_mod, b_mod, norm_g, w_in = map(f, (c_ctx, w_mod, b_mod, norm_g, w_in))
    w_gk_f, b_gk_f, w_gk_b, b_gk_b, sink, gla_norm_g, w_out, final_norm_g = map(
        f, (w_gk_f, b_gk_f, w_gk_b, b_gk_b, sink, gla_norm_g, w_out, final_norm_g))

    if "nc" not in _CACHE:
        _CACHE["nc"] = build_program()
    nc = _CACHE["nc"]

    wi = w_in[0]
    o_q, o_k, o_v, o_za, o_qg, o_kg, o_vg, o_lrf, o_lrb, o_zg = 0, 512, 640, 768, 1280, 1536, 1792, 2304, 2320, 2336
    win_p = np.concatenate([wi[:, o_vg:o_vg + 512], wi[:, o_kg:o_kg + 256], wi[:, o_k:o_k + 128], wi[:, o_v:o_v + 128],
                            wi[:, o_lrf:o_lrf + 16], wi[:, o_lrb:o_lrb + 16],
                            wi[:, o_q:o_q + 512], wi[:, o_za:o_za + 512], wi[:, o_zg:o_zg + 512], wi[:, o_qg:o_qg + 256]], axis=1)
    win_p = np.ascontiguousarray(win_p)

    def wgk_aug(wf, bf_, wb, bb_):
        m = np.zeros((33, 512), np.float32)
        m[0:16, 0:256] = wf
        m[16:32, 256:512] = wb
        m[32, 0:256] = bf_
        m[32, 256:512] = bb_
        return m

    wgk_ctx = wgk_aug(w_gk_f[0], b_gk_f[0], w_gk_b[0], b_gk_b[0])
    wgk_swp = wgk_aug(w_gk_b[0], b_gk_b[0], w_gk_f[0], b_gk_f[0])
    tri, chsel, gmask, wmask, ident = _consts()

    in_maps = []
    for core in range(NCORES):
        b, h = core // 2, core % 2
        if h == 0:
            xs_l = x_sample[b]
            pos = np.arange(17 * 128)
            wlrs = np.concatenate([wi[:, o_lrf:o_lrf + 16], wi[:, o_lrb:o_lrb + 16]], 1)
            wgk_s = wgk_ctx
            st = state_gla[b, 0]
        else:
            xs_l = x_sample[b, ::-1]
            pos = 4095 - np.arange(17 * 128)
            wlrs = np.concatenate([wi[:, o_lrb:o_lrb + 16], wi[:, o_lrf:o_lrf + 16]], 1)
            wgk_s = wgk_swp
            st = state_gla[b, 0, ::-1]
        in_maps.append({
            "xp": np.ascontiguousarray(x_prompt[core * 4:(core + 1) * 4]),
            "xs": np.ascontiguousarray(xs_l),
            "cv": np.ascontiguousarray(np.stack([c_ctx, c[b]])),
            "ck": np.ascontiguousarray(cache_k[b, 0].reshape(512, 128)),
            "cvv": np.ascontiguousarray(cache_v[b, 0].reshape(512, 128)),
            "st": np.ascontiguousarray(st),
            "wmod": w_mod[0], "bmod": b_mod[0], "ng": norm_g[0],
            "win": win_p, "wlrs": np.ascontiguousarray(wlrs),
            "wgk": np.ascontiguousarray(np.stack([wgk_ctx, wgk_s])),
            "sink": sink[0], "gng": gla_norm_g[0], "wout": w_out[0], "fng": final_norm_g,
            "ident": ident, "tri": tri, "chsel": chsel, "gmask": gmask, "wmask": wmask,
            "rope": _rope_tables(pos),
        })
    if _CACHE.get("sim_hook") is not None:
        return _CACHE["sim_hook"](nc, in_maps)
    res = run_bass_kernel_spmd(nc, in_maps, core_ids=list(range(NCORES)))
    R = res.results
    y_prompt = np.concatenate([R[i]["yp"] for i in range(NCORES)], 0)
    new_k = np.concatenate([R[i]["nk"] for i in range(NCORES)], 0).reshape(32, 1, 256, 2, 64)
    new_v = np.concatenate([R[i]["nv"] for i in range(NCORES)], 0).reshape(32, 1, 256, 2, 64)
    new_state = np.concatenate([R[i]["ns"] for i in range(NCORES)], 0).reshape(32, 1, 2, 4, 64, 128)
    y_sample = np.empty((4, 4096, D), np.float32)
    for core in range(NCORES):
        b, h = core // 2, core % 2
        if h == 0:
            y_sample[b, 0:2048] = R[core]["ys"]
        else:
            y_sample[b, 2048:4096] = R[core]["ys"][::-1]
    return (y_prompt, y_sample, new_k, new_v, new_state)
```
